# Optimizing a Trainium2 kernel written in Bass

```python
import jax, jax.numpy as jnp
from jax import lax
import numpy as np

D_MODEL = 1024
BATCH = 2
SEQ = 8192
DEPTH = 4
DEC_BATCH = 128
DEC_SEQ = 4
PAST_LEN = 8192
PAGE_SIZE = 128

N_HEADS = 8
NOPE_DIM = D_MODEL // 16
ROPE_DIM = D_MODEL // 32
V_DIM = D_MODEL // 16
Q_LORA = 12 * NOPE_DIM
KV_LORA = 4 * NOPE_DIM
ROPE_BASE = 10000.0
ATTN_SCALE = (NOPE_DIM + ROPE_DIM) ** -0.5
Q_BLOCK = 128
CONV_CH = D_MODEL // 2
CONV_WIDTH = 31
POOL_WINDOWS = (2, 4, 8, 16)
N_POOL_GROUPS = len(POOL_WINDOWS)
POOL_GROUP = D_MODEL // N_POOL_GROUPS
POOL_STATE = max(POOL_WINDOWS) - 1
D_FF = ((8 * D_MODEL // 3 + 127) // 128) * 128
EPS = 1e-6
IN_WIDTH = Q_LORA + KV_LORA + ROPE_DIM + 2 * CONV_CH
MIX_WIDTH = N_HEADS * V_DIM + CONV_CH
N_EVEN = (DEPTH + 1) // 2
N_ODD = DEPTH // 2

kernel_name = 'hybrid_mla_conv_pool_decoder'


def rmsnorm(x, g):
    xf = x.astype(jnp.float32)
    y = xf * lax.rsqrt(jnp.mean(xf * xf, -1, keepdims=True) + EPS)
    return (y * g.astype(jnp.float32)).astype(x.dtype)


def layernorm(x, g, b):
    xf = x.astype(jnp.float32)
    xc = xf - jnp.mean(xf, -1, keepdims=True)
    var = jnp.mean(xc * xc, -1, keepdims=True)
    return (xc * lax.rsqrt(var + EPS) * g.astype(jnp.float32) + b.astype(jnp.float32)).astype(x.dtype)


def swiglu(x, wg, wu, wd):
    return (jax.nn.silu(x @ wg) * (x @ wu)) @ wd


def rope_tables(pos):
    inv = ROPE_BASE ** (-jnp.arange(0, ROPE_DIM, 2, dtype=jnp.float32) / ROPE_DIM)
    ang = pos[:, None] * inv[None, :]
    return jnp.cos(ang), jnp.sin(ang)


def apply_rope(x, cos, sin):
    xf = x.astype(jnp.float32)
    half = ROPE_DIM // 2
    x1, x2 = xf[..., :half], xf[..., half:]
    return jnp.concatenate([x1 * cos - x2 * sin, x2 * cos + x1 * sin], -1).astype(x.dtype)


def mla_project(h, cos, sin, q_norm, w_uq, kv_norm, w_uk):
    q_lat = h[..., :Q_LORA]
    kv_lat = h[..., Q_LORA:Q_LORA + KV_LORA]
    kr = h[..., Q_LORA + KV_LORA:Q_LORA + KV_LORA + ROPE_DIM]
    conv_in = h[..., Q_LORA + KV_LORA + ROPE_DIM:]
    q = jnp.einsum('btq,qhd->bthd', rmsnorm(q_lat, q_norm), w_uq)
    q_nope, q_rope = q[..., :NOPE_DIM], q[..., NOPE_DIM:]
    q_rope = apply_rope(q_rope, cos[:, None, :], sin[:, None, :])
    q_abs = jnp.einsum('bthd,chd->bthc', q_nope, w_uk)
    c_kv = rmsnorm(kv_lat, kv_norm)
    k_rope = apply_rope(kr, cos, sin)
    return q_abs, q_rope, c_kv, k_rope, conv_in


def mla_attend_prompt(q_abs, q_rope, c_kv, k_rope):
    B, T = q_abs.shape[:2]
    nb = T // Q_BLOCK
    qa = q_abs.reshape(B, nb, Q_BLOCK, N_HEADS, KV_LORA).transpose(1, 0, 2, 3, 4)
    qr = q_rope.reshape(B, nb, Q_BLOCK, N_HEADS, ROPE_DIM).transpose(1, 0, 2, 3, 4)
    kpos = jnp.arange(T)

    def block(args):
        qa_b, qr_b, i = args
        s = (jnp.einsum('bqhc,bkc->bhqk', qa_b, c_kv)
             + jnp.einsum('bqhr,bkr->bhqk', qr_b, k_rope)).astype(jnp.float32) * ATTN_SCALE
        qpos = i * Q_BLOCK + jnp.arange(Q_BLOCK)
        s = jnp.where(kpos[None, :] <= qpos[:, None], s, -jnp.inf)
        p = jax.nn.softmax(s, -1).astype(c_kv.dtype)
        return jnp.einsum('bhqk,bkc->bqhc', p, c_kv)

    o = lax.map(block, (qa, qr, jnp.arange(nb)))
    return o.transpose(1, 0, 2, 3, 4).reshape(B, T, N_HEADS, KV_LORA)


def mla_attend_sample(q_abs, q_rope, c_kv, k_rope, lat_past, kr_past):
    TS = q_abs.shape[1]
    P = lat_past.shape[1]
    s_past = (jnp.einsum('bthc,bkc->bhtk', q_abs, lat_past)
              + jnp.einsum('bthr,bkr->bhtk', q_rope, kr_past)).astype(jnp.float32)
    s_new = (jnp.einsum('bthc,bkc->bhtk', q_abs, c_kv)
             + jnp.einsum('bthr,bkr->bhtk', q_rope, k_rope)).astype(jnp.float32)
    causal = jnp.tril(jnp.ones((TS, TS), dtype=bool))
    s_new = jnp.where(causal, s_new, -jnp.inf)
    p = jax.nn.softmax(jnp.concatenate([s_past, s_new], -1) * ATTN_SCALE, -1).astype(c_kv.dtype)
    return (jnp.einsum('bhtk,bkc->bthc', p[..., :P], lat_past)
            + jnp.einsum('bhtk,bkc->bthc', p[..., P:], c_kv))


def conformer_conv(conv_in, hist, w, b, ln_g, ln_b):
    a, g = conv_in[..., :CONV_CH], conv_in[..., CONV_CH:]
    u = a * jax.nn.sigmoid(g)
    full = jnp.concatenate([hist, u], 1)
    y = lax.conv_general_dilated(full, w[:, None, :], window_strides=(1,), padding='VALID',
                                 dimension_numbers=('NWC', 'WIO', 'NWC'),
                                 feature_group_count=CONV_CH) + b
    y = jax.nn.silu(layernorm(y, ln_g, ln_b))
    return y, full[:, -(CONV_WIDTH - 1):]


def pool_mixer(h, hist, pos0, pool_w, pool_scale):
    B, T, D = h.shape
    S = POOL_STATE
    full = jnp.concatenate([hist, h], 1)
    ff = full.astype(jnp.float32)
    cs = jnp.concatenate([jnp.zeros((B, 1, D), jnp.float32), jnp.cumsum(ff, 1)], 1)
    cnt_pos = pos0 + jnp.arange(T) + 1
    outs = []
    for gi, w in enumerate(POOL_WINDOWS):
        sl = slice(gi * POOL_GROUP, (gi + 1) * POOL_GROUP)
        s = cs[:, S + 1:S + 1 + T, sl] - cs[:, S + 1 - w:S + 1 - w + T, sl]
        cnt = jnp.minimum(cnt_pos, w).astype(jnp.float32)[None, :, None]
        outs.append(s / cnt)
    diff = (jnp.concatenate(outs, -1) - ff[:, S:]).astype(h.dtype)
    y = jnp.einsum('btgc,gcd->btgd', diff.reshape(B, T, N_POOL_GROUPS, POOL_GROUP), pool_w)
    return y.reshape(B, T, D) * pool_scale, full[:, -S:]


def setup_inputs(seed: int = 0) -> dict:
    key = jax.random.key(seed)
    keys = iter(jax.random.split(key, 64))

    def nrm(shape, scale):
        return jax.random.normal(next(keys), shape, jnp.float32) * scale

    def gain(shape):
        return 1.0 + 0.05 * jax.random.normal(next(keys), shape, jnp.float32)

    n_pages = PAST_LEN // PAGE_SIZE
    n_used = DEC_BATCH * n_pages
    n_phys = (5 * n_used + 3) // 4
    perm = jax.random.permutation(next(keys), n_phys)
    page_table = perm[:n_used].reshape(DEC_BATCH, n_pages).astype(jnp.int32)
    ne, no = N_EVEN, N_ODD
    return {
        'x_prompt': nrm((BATCH, SEQ, D_MODEL), 1.0),
        'x_sample': nrm((DEC_BATCH, DEC_SEQ, D_MODEL), 1.0),
        'cache_latent': nrm((ne, n_phys, PAGE_SIZE, KV_LORA), 1.0),
        'cache_krope': nrm((ne, n_phys, PAGE_SIZE, ROPE_DIM), 1.0),
        'state_conv': nrm((ne, DEC_BATCH, CONV_WIDTH - 1, CONV_CH), 0.5),
        'state_pool': nrm((no, DEC_BATCH, POOL_STATE, D_MODEL), 1.0),
        'page_table': page_table,
        'ffn1_norm': gain((DEPTH, D_MODEL)),
        'ffn1_w_gate': nrm((DEPTH, D_MODEL, D_FF), D_MODEL ** -0.5),
        'ffn1_w_up': nrm((DEPTH, D_MODEL, D_FF), D_MODEL ** -0.5),
        'ffn1_w_down': nrm((DEPTH, D_FF, D_MODEL), D_FF ** -0.5),
        'mix_norm': gain((DEPTH, D_MODEL)),
        'w_in': nrm((ne, D_MODEL, IN_WIDTH), D_MODEL ** -0.5),
        'q_norm': gain((ne, Q_LORA)),
        'w_uq': nrm((ne, Q_LORA, N_HEADS, NOPE_DIM + ROPE_DIM), Q_LORA ** -0.5),
        'kv_norm': gain((ne, KV_LORA)),
        'w_uk': nrm((ne, KV_LORA, N_HEADS, NOPE_DIM), KV_LORA ** -0.5),
        'w_uv': nrm((ne, KV_LORA, N_HEADS, V_DIM), KV_LORA ** -0.5),
        'conv_w': nrm((ne, CONV_WIDTH, CONV_CH), CONV_WIDTH ** -0.5),
        'conv_b': nrm((ne, CONV_CH), 0.02),
        'conv_ln_g': gain((ne, CONV_CH)),
        'conv_ln_b': nrm((ne, CONV_CH), 0.02),
        'w_out': nrm((ne, MIX_WIDTH, D_MODEL), MIX_WIDTH ** -0.5),
        'pool_w': nrm((no, N_POOL_GROUPS, POOL_GROUP, POOL_GROUP), POOL_GROUP ** -0.5),
        'pool_scale': gain((no, D_MODEL)),
        'ffn2_norm': gain((DEPTH, D_MODEL)),
        'ffn2_w_gate': nrm((DEPTH, D_MODEL, D_FF), D_MODEL ** -0.5),
        'ffn2_w_up': nrm((DEPTH, D_MODEL, D_FF), D_MODEL ** -0.5),
        'ffn2_w_down': nrm((DEPTH, D_FF, D_MODEL), D_FF ** -0.5),
        'final_norm': gain((D_MODEL,)),
    }


def reference(x_prompt, x_sample, cache_latent, cache_krope, state_conv, state_pool, page_table,
              ffn1_norm, ffn1_w_gate, ffn1_w_up, ffn1_w_down,
              mix_norm, w_in, q_norm, w_uq, kv_norm, w_uk, w_uv,
              conv_w, conv_b, conv_ln_g, conv_ln_b, w_out,
              pool_w, pool_scale,
              ffn2_norm, ffn2_w_gate, ffn2_w_up, ffn2_w_down, final_norm):
    B, T = x_prompt.shape[:2]
    DB, TS = x_sample.shape[:2]
    n_pages = page_table.shape[1]
    past_len = n_pages * cache_latent.shape[2]
    cos_p, sin_p = rope_tables(jnp.arange(T, dtype=jnp.float32))
    cos_s, sin_s = rope_tables(past_len + jnp.arange(TS, dtype=jnp.float32))
    conv_hist0 = jnp.zeros((B, CONV_WIDTH - 1, CONV_CH), x_prompt.dtype)
    pool_hist0 = jnp.zeros((B, POOL_STATE, D_MODEL), x_prompt.dtype)

    xp, xs = x_prompt, x_sample
    lat_p, kr_p, lat_s, kr_s = [], [], [], []
    conv_p, conv_s, pool_p, pool_s = [], [], [], []
    for l in range(DEPTH):
        xp = xp + 0.5 * swiglu(rmsnorm(xp, ffn1_norm[l]), ffn1_w_gate[l], ffn1_w_up[l], ffn1_w_down[l])
        xs = xs + 0.5 * swiglu(rmsnorm(xs, ffn1_norm[l]), ffn1_w_gate[l], ffn1_w_up[l], ffn1_w_down[l])
        if l % 2 == 0:
            e = l // 2
            hp = rmsnorm(xp, mix_norm[l]) @ w_in[e]
            hs = rmsnorm(xs, mix_norm[l]) @ w_in[e]
            qa_p, qr_p, ckv_p, kro_p, cin_p = mla_project(hp, cos_p, sin_p, q_norm[e], w_uq[e], kv_norm[e], w_uk[e])
            qa_s, qr_s, ckv_s, kro_s, cin_s = mla_project(hs, cos_s, sin_s, q_norm[e], w_uq[e], kv_norm[e], w_uk[e])
            o_p = mla_attend_prompt(qa_p, qr_p, ckv_p, kro_p)
            lat_past = cache_latent[e][page_table].reshape(DB, past_len, KV_LORA)
            kr_past = cache_krope[e][page_table].reshape(DB, past_len, ROPE_DIM)
            o_s = mla_attend_sample(qa_s, qr_s, ckv_s, kro_s, lat_past, kr_past)
            att_p = jnp.einsum('bthc,chv->bthv', o_p, w_uv[e]).reshape(B, T, N_HEADS * V_DIM)
            att_s = jnp.einsum('bthc,chv->bthv', o_s, w_uv[e]).reshape(DB, TS, N_HEADS * V_DIM)
            cv_p, hist_p = conformer_conv(cin_p, conv_hist0, conv_w[e], conv_b[e], conv_ln_g[e], conv_ln_b[e])
            cv_s, hist_s = conformer_conv(cin_s, state_conv[e], conv_w[e], conv_b[e], conv_ln_g[e], conv_ln_b[e])
            xp = xp + jnp.concatenate([att_p, cv_p], -1) @ w_out[e]
            xs = xs + jnp.concatenate([att_s, cv_s], -1) @ w_out[e]
            lat_p.append(ckv_p); kr_p.append(kro_p)
            lat_s.append(ckv_s); kr_s.append(kro_s)
            conv_p.append(hist_p); conv_s.append(hist_s)
        else:
            o = l // 2
            yp, ph = pool_mixer(rmsnorm(xp, mix_norm[l]), pool_hist0, 0, pool_w[o], pool_scale[o])
            ys, sh = pool_mixer(rmsnorm(xs, mix_norm[l]), state_pool[o], past_len, pool_w[o], pool_scale[o])
            xp = xp + yp
            xs = xs + ys
            pool_p.append(ph); pool_s.append(sh)
        xp = xp + 0.5 * swiglu(rmsnorm(xp, ffn2_norm[l]), ffn2_w_gate[l], ffn2_w_up[l], ffn2_w_down[l])
        xs = xs + 0.5 * swiglu(rmsnorm(xs, ffn2_norm[l]), ffn2_w_gate[l], ffn2_w_up[l], ffn2_w_down[l])

    y_prompt = rmsnorm(xp, final_norm)
    y_sample = rmsnorm(xs, final_norm)
    return (y_prompt, y_sample,
            jnp.stack(lat_p), jnp.stack(kr_p), jnp.stack(lat_s), jnp.stack(kr_s),
            jnp.stack(conv_p), jnp.stack(conv_s), jnp.stack(pool_p), jnp.stack(pool_s))
```

```python
from contextlib import ExitStack
import numpy as np
import concourse.bass as bass
import concourse.mybir as mybir
from concourse.bass_utils import run_bass_kernel_spmd

F32 = mybir.dt.float32
BF16 = mybir.dt.bfloat16
I32 = mybir.dt.int32
ALU = mybir.AluOpType
AF = mybir.ActivationFunctionType

D = 1024
DC = 8
DFF = 2816
FC = 22
QL = 768
KVL = 256
ROPE = 32
NH = 8
NOPE = 64
VD = 64
CCH = 512
CW = 31
INW = 2080
PS = 15
EPS = 1e-6
DEPTH = 4
PAGE = 128
SCALE = float((NOPE + ROPE) ** -0.5)


class Cfg:
    def __init__(self, seq=8192, nt=512, dseq=16, dtok=4, npages=64, nphys=10240, ncores=8):
        self.seq, self.nt, self.dseq, self.dtok, self.npages, self.nphys = seq, nt, dseq, dtok, npages, nphys
        self.ncores = ncores
        self.ntiles = seq // nt
        self.ns = dseq * dtok
        self.parts = 'fpa'


class Prog:
    def __init__(self, nc, stack):
        self.nc, self.stack = nc, stack
        self.engs = ["sync", "scalar", "gpsimd", "vector", "tensor"]
        self.stream = {e: [] for e in self.engs}
        self.cmp = {}
        self.dmapool = {}
        self.dmaidx = {}
        self.seen = {e: {} for e in self.engs}
        self.bufs = {}
        self.nsem = 0
        self.last = {}

    def _sem(self):
        self.nsem += 1
        return self.stack.enter_context(self.nc.semaphore(f"sem{self.nsem}"))

    def op(self, e, meth, reads=(), writes=(), dma=False, **kw):
        fn = (meth, kw)
        waits = {}
        reads = list(reads)
        writes = list(writes)
        if "PHASE" not in writes:
            reads.append("PHASE")

        def need(t):
            if t is None:
                return
            s, v = t
            k = id(s)
            if self.seen[e].get(k, 0) >= v:
                return
            if k not in waits or waits[k][1] < v:
                waits[k] = (s, v)

        if dma:
            pool = self.dmapool.setdefault(e, [])
            i = self.dmaidx.get(e, 0)
            self.dmaidx[e] = i + 1
            slot = i % 8
            if len(pool) <= slot:
                pool.append([self._sem(), 0])
            if pool[slot][1] > 30000:
                need((pool[slot][0], pool[slot][1]))
                cur = pool[slot] = [self._sem(), 0]
            else:
                cur = pool[slot]
                need((cur[0], cur[1]) if cur[1] else None)
            inc = 16
        else:
            if e not in self.cmp or self.cmp[e][1] > 30000:
                self.cmp[e] = [self._sem(), 0]
            cur = self.cmp[e]
            inc = 1
        for b in reads:
            need(self.bufs.setdefault(b, {"w": None, "r": {}})["w"])
        for b in writes:
            d = self.bufs.setdefault(b, {"w": None, "r": {}})
            need(d["w"])
            for t in d["r"].values():
                need(t)
        for k, (s, v) in waits.items():
            self.seen[e][k] = v
        cur[1] += inc
        tk = (cur[0], cur[1])
        self.stream[e].append((list(waits.values()), fn, tk[0], inc))
        for b in reads:
            self.bufs[b]["r"][id(tk[0])] = tk
        for b in writes:
            self.bufs[b]["w"] = tk
            self.bufs[b]["r"] = {}
        self.last[e] = tk
        return tk

    def finish(self, e, tickets):
        best = {}
        for s_, v in tickets:
            if id(s_) not in best or best[id(s_)][1] < v:
                best[id(s_)] = (s_, v)
        self.stream[e].append((list(best.values()), None, None, 0))

    def emit(self, block):
        def mk(e):
            def body(eng):
                for waits, fn, sem, inc in self.stream[e]:
                    for s, v in waits:
                        eng.wait_ge(s, v)
                    if fn is not None:
                        getattr(eng, fn[0])(**fn[1]).then_inc(sem, inc)
            return body
        for e in self.engs:
            if self.stream[e]:
                getattr(block, e)(mk(e))


def build(cfg, n_layers=DEPTH):
    nc = bass.Bass("TRN2", target_bir_lowering=False)
    try:
        nc.allow_low_precision("bf16 matmul operands with fp32 accumulation")
    except Exception:
        pass
    NT, SEQ, NS = cfg.nt, cfg.seq, cfg.ns
    NTOK = SEQ + NS

    def din(name, shape, dt=F32):
        return nc.dram_tensor(name, list(shape), dt, kind="ExternalInput").ap()

    def dout(name, shape, dt=F32):
        return nc.dram_tensor(name, list(shape), dt, kind="ExternalOutput").ap()

    xT = din("xT", [D, NTOK])
    norms = din("norms", [128, 13, DC])
    wg = din("wg", [2 * DEPTH, D, DFF])
    wu = din("wu", [2 * DEPTH, D, DFF])
    wd = din("wd", [2 * DEPTH, DFF, D])
    pool_w = din("pool_w", [2, 4, 256, 256])
    pool_sc = din("pool_sc", [128, 2, DC])
    pool_hist = din("pool_hist", [2, D, cfg.dseq, PS])
    invcnt = din("invcnt", [128, 4, NT])
    w_in = din("w_in", [2, D, INW])
    qng_d = din("qng", [128, 2, 6])
    kvng_d = din("kvng", [128, 2, 2])
    w_uq = din("w_uq", [2, QL, 768])
    w_ukT = din("w_ukT", [2, NH, NOPE, KVL])
    w_uv = din("w_uv", [2, KVL, NH * VD])
    cw_d = din("cw", [128, 2, 4, CW])
    cvec_d = din("cvec", [128, 3, 2, 4])
    w_out = din("w_out", [2, D, D])
    cs_d = din("cossin", [2, 48, NTOK])
    conv_hist = din("conv_hist", [2, CCH, cfg.dseq, CW - 1])
    amask_d = din("amask", [128, NT // 128, NT])
    ident_d = din("ident", [128, 128])
    cache_lat = [din(f"cache_lat{i}", [cfg.nphys * PAGE, KVL]) for i in range(2)]
    cache_kr = [din(f"cache_kr{i}", [cfg.nphys * PAGE, ROPE]) for i in range(2)]
    pt_d = din("pt_rep", [128, cfg.dseq * cfg.npages], I32)
    riota_d = din("rowiota", [128, 1])
    smask_d = din("smask", [4, NH * 4])
    latT = dout("latT", [2, KVL, NTOK])
    krT = dout("krT", [2, ROPE, NTOK])
    convT = dout("convT", [2, CCH, (CW - 1) * (1 + cfg.dseq)])
    kcT = nc.dram_tensor("kcT", [2, 128, 2, SEQ], BF16, kind="Internal").ap()
    kcR = nc.dram_tensor("kcR", [2, 48, SEQ], BF16, kind="Internal").ap()
    kcA = nc.dram_tensor("kcA", [2, SEQ // 128, 128, KVL], BF16, kind="Internal").ap()
    yT = dout("yT", [D, NTOK])
    poolT = dout("poolT", [2, D, PS + cfg.dseq * PS])

    stack = ExitStack()
    with stack:
        pr = Prog(nc, stack)

        def sb(name, shape, dt=F32):
            return stack.enter_context(nc.sbuf_tensor(name, list(shape), dt))

        def ps(name, shape, dt=F32):
            return stack.enter_context(nc.psum_tensor(name, list(shape), dt))

        x = sb("x", [128, DC, NT])
        xn = sb("xn", [128, DC, NT], BF16)
        sq = sb("sq", [128, DC, NT])
        rstd = sb("rstd", [128, NT])
        ones = sb("ones", [128, 128])
        nrm = sb("nrm", [128, 13, DC])
        wg_st = sb("wg_st", [128, DC, 256])
        wu_st = sb("wu_st", [128, DC, 256])
        wd_st = sb("wd_st", [128, 2, D])
        wg_b = sb("wg_b", [128, DC, 256], BF16)
        wu_b = sb("wu_b", [128, DC, 256], BF16)
        wd_b = sb("wd_b", [128, 2, D], BF16)
        hs = sb("hs", [128, 2, NT])
        hb = sb("hb", [128, 2, NT], BF16)
        phalo = sb("phalo", [128, 2, DC, PS])
        psc = sb("psc", [128, 2, DC])
        icnt = sb("icnt", [128, 4, NT])
        pw_st = sb("pw_st", [128, 2, 256])
        pw_b = sb("pw_b", [128, 4, 2, 256], BF16)

        PA = ps("PA", [128, 8, 256])
        PB = ps("PB", [128, 512])
        PC = ps("PC", [128, 512])
        PD = ps("PD", [128, 512])
        PT_ = ps("PT_", [128, 512], BF16)
        ps_r, ps_g, ps_u, ps_y = PB, PC, PD, PA

        QB = NT // 128
        HW_ = CW - 1
        qlat = sb("qlat", [128, 6, NT])
        qn = sb("qn", [128, 6, NT], BF16)
        kvlat = sb("kvlat", [128, 2, NT])
        ckv32 = sb("ckv32", [128, 2, NT])
        ckvb = sb("ckvb", [128, 2, NT], BF16)
        kab = sb("kab", [48, 2, NT])
        kt1 = sb("kt1", [48, NT])
        kt2 = sb("kt2", [48, NT])
        kr32 = sb("kr32", [48, NT])
        krb = sb("krb", [48, NT], BF16)
        cs = sb("cs", [48, 2, NT])
        PW = max(PS + NT, cfg.dseq * (PS + cfg.dtok))
        ag = sb("ag", [128, 8, max(NT, PW)])
        uT = sb("uT", [128, 4, max(HW_ + NT, cfg.dseq * (HW_ + cfg.dtok), 2 * PW)])
        chalo = sb("chalo", [128, 2, 4, HW_])
        cacc = sb("cacc", [128, 4, NT])
        cmean = sb("cmean", [128, NT])
        cv = sb("cv", [128, 4, NT], BF16)
        qnope = sb("qnope", [64, NH, NT], BF16)
        qabsT = sb("qabsT", [128, 2, NH, NT], BF16)
        rq32 = sb("rq32", [48, NH, NT])
        rqb = sb("rqb", [48, NH, NT], BF16)
        wq_b = sb("wq_b", [128, 6, 192], BF16)
        wuk_b = sb("wuk_b", [64, NH, KVL], BF16)
        wuv_b = sb("wuv_b", [128, 2, NH * VD], BF16)
        woa_b = sb("woa_b", [64, NH, 128], BF16)
        woc_b = sb("woc_b", [128, 4, 128], BF16)
        qng = sb("qng_s", [128, 2, 6])
        kvng = sb("kvng_s", [128, 2, 2])
        cw = sb("cw_s", [128, 2, 4, CW])
        cvec = sb("cvec_s", [128, 3, 2, 4])
        amask = sb("amask_s", [128, QB, NT])
        ident32 = sb("ident32", [128, 128])
        identb = sb("identb", [128, 128], BF16)
        ones2 = sb("ones2", [128, 2], BF16)
        kaug = sb("kaug", [128, QB, KVL], BF16)
        kbT = sb("kbT", [128, 2, 128], BF16)
        kbR = sb("kbR", [48, 128], BF16)
        kbA = sb("kbA", [128, KVL], BF16)
        PTs = sb("PTs", [128, NT], BF16)
        rec = sb("rec", [128, NH * QB])
        Oacc = sb("Oacc", [128, NH * QB, KVL])
        Lacc = sb("Lacc", [128, NH * QB])
        assert NH * QB * KVL >= DC * PW
        pf = Oacc[:].rearrange("p a b -> p (a b)")[:, 0:DC * PW].rearrange("p (k w) -> p k w", k=DC)
        pa = ag[:, :, 0:PW]
        pb = uT[:].rearrange("p a b -> p (a b)")[:, 0:DC * PW].rearrange("p (k w) -> p k w", k=DC)
        pdiff = qabsT[:].rearrange("p a b c -> p (a b c)")[:, 0:DC * NT].rearrange("p (k w) -> p k w", k=DC)
        sview = lambda buf: buf[:, :, 0:cfg.dseq * (PS + cfg.dtok)].rearrange("p k (s c) -> p k s c", c=PS + cfg.dtok)
        spf, spa, spb = sview(pf), sview(pa), sview(pb)
        Onb = sb("Onb", [128, KVL], BF16)
        OT = sb("OT", [128, 2, NH, NT], BF16)
        attT = sb("attT", [64, NH, NT], BF16)

        NPG = cfg.dseq * cfg.npages
        pt_i = sb("pt_i", [128, NPG], I32)
        pt_f = sb("pt_f", [128, NPG])
        idx32 = sb("idx32", [128, NPG], I32)
        riota = sb("riota", [128, 1])
        smask = sb("smask_s", [4, NH * 4])
        lp32 = sb("lp32", [128, KVL])
        rp32 = sb("rp32", [128, ROPE])
        lpb = sb("lpb", [128, KVL], BF16)
        rp48 = sb("rp48", [128, 48], BF16)
        knew = sb("knew", [4, KVL], BF16)

        out_tickets = []

        def V(m, r, w, **kw): return pr.op("vector", m, r, w, **kw)
        def G(m, r, w, **kw): return pr.op("gpsimd", m, r, w, **kw)
        def A(m, r, w, **kw): return pr.op("scalar", m, r, w, **kw)
        def T(r, w, **kw): return pr.op("tensor", "matmul", r, w, **kw)
        def DMA(r, w, **kw): return pr.op("sync", "dma_start", r, w, dma=True, **kw)

        V("memset", [], ["ones"], ap=ones[:], constant=1.0)
        DMA([], ["nrm"], out=nrm[:], in_=norms)
        DMA([], ["psc"], out=psc[:], in_=pool_sc)
        DMA([], ["icnt"], out=icnt[:], in_=invcnt)
        V("memset", [], ["phalo"], ap=phalo[:], constant=0.0)

        def rmsnorm(n, which, dst, dst_name):
            G("tensor_tensor", ["x"], ["sq"], out=sq[:, :, :n], in0=x[:, :, :n], in1=x[:, :, :n], op=ALU.mult)
            for k in range(DC):
                T(["ones", "sq"], ["PB"], out=ps_r[:, :n], lhsT=ones[:], rhs=sq[:, k, :n], start=(k == 0), stop=(k == DC - 1))
            V("tensor_scalar", ["PB"], ["rstd"], out=rstd[:, :n], in0=ps_r[:, :n], scalar1=1.0 / D, scalar2=EPS,
              op0=ALU.mult, op1=ALU.add)
            A("activation", ["rstd"], ["rstd"], out=rstd[:, :n], in_=rstd[:, :n], func=AF.Sqrt)
            V("reciprocal", ["rstd"], ["rstd"], out=rstd[:, :n], in_=rstd[:, :n])
            for k in range(DC):
                V("scalar_tensor_tensor", ["x", "nrm", "rstd"], [dst_name], out=dst(k), in0=x[:, k, :n],
                  scalar=nrm[:, which, k:k + 1], in1=rstd[:, :n], op0=ALU.mult, op1=ALU.mult)

        def ffn(n, widx, which):
            rmsnorm(n, which, lambda k: xn[:, k, :n], "xn")
            for g in range(FC // 2):
                f0 = g * 256
                DMA([], ["wg_st"], out=wg_st[:], in_=wg[widx].rearrange("(kc p) f -> p kc f", p=128)[:, :, f0:f0 + 256])
                DMA([], ["wu_st"], out=wu_st[:], in_=wu[widx].rearrange("(kc p) f -> p kc f", p=128)[:, :, f0:f0 + 256])
                DMA([], ["wd_st"], out=wd_st[:], in_=wd[widx, f0:f0 + 256, :].rearrange("(c p) d -> p c d", p=128))
                G("tensor_copy", ["wg_st"], ["wg_b"], out=wg_b[:], in_=wg_st[:])
                A("copy", ["wu_st"], ["wu_b"], out=wu_b[:], in_=wu_st[:])
                G("tensor_copy", ["wd_st"], ["wd_b"], out=wd_b[:], in_=wd_st[:])
                for c in range(2):
                    for k in range(DC):
                        T(["wg_b", "xn"], ["PC"], out=ps_g[:, :n], lhsT=wg_b[:, k, c * 128:(c + 1) * 128], rhs=xn[:, k, :n],
                          start=(k == 0), stop=(k == DC - 1))
                    for k in range(DC):
                        T(["wu_b", "xn"], ["PD"], out=ps_u[:, :n], lhsT=wu_b[:, k, c * 128:(c + 1) * 128], rhs=xn[:, k, :n],
                          start=(k == 0), stop=(k == DC - 1))
                    A("activation", ["PC"], ["hs"], out=hs[:, c, :n], in_=ps_g[:, :n], func=AF.Silu)
                    V("tensor_tensor", ["hs", "PD"], ["hb"], out=hb[:, c, :n], in0=hs[:, c, :n], in1=ps_u[:, :n], op=ALU.mult)
                for half in range(2):
                    for dc4 in range(4):
                        dc = half * 4 + dc4
                        for c in range(2):
                            T(["wd_b", "hb"], ["PA"], out=ps_y[:, dc4, :n], lhsT=wd_b[:, c, dc * 128:(dc + 1) * 128],
                              rhs=hb[:, c, :n], start=(c == 0), stop=(c == 1))
                    for dc4 in range(4):
                        dc = half * 4 + dc4
                        V("scalar_tensor_tensor", ["PA", "x"], ["x"], out=x[:, dc, :n], in0=ps_y[:, dc4, :n], scalar=0.5,
                          in1=x[:, dc, :n], op0=ALU.mult, op1=ALU.add)

        def pool_layer(n, o, lidx, kind, t):
            sample = kind == "s"
            barrier()
            if not sample:
                F, A_, B_, Fn, An, Bn = pf, pa, pb, "Oacc", "ag", "uT"
                W = PS + n
                fv = lambda buf, k, lo, hi: buf[:, k, lo:hi]
                G("tensor_copy", ["phalo"], ["Oacc"], out=pf[:, :, 0:PS], in_=phalo[:, o, :, :])
                rmsnorm(n, 4 + lidx, lambda k: pf[:, k, PS:PS + n], "Oacc")
                G("tensor_copy", ["Oacc"], ["phalo"], out=phalo[:, o, :, :], in_=pf[:, :, n:n + PS])
                if t == cfg.ntiles - 1:
                    for k in range(DC):
                        out_tickets.append(DMA(["Oacc"], [], out=poolT[o, k * 128:(k + 1) * 128, 0:PS], in_=pf[:, k, n:n + PS]))
            else:
                F, A_, B_, Fn, An, Bn = spf, spa, spb, "Oacc", "ag", "uT"
                W = PS + cfg.dtok
                fv = lambda buf, k, lo, hi: buf[:, k, :, lo:hi]
                for k in range(DC):
                    DMA([], ["Oacc"], out=spf[:, k, :, 0:PS], in_=pool_hist[o, k * 128:(k + 1) * 128, :, :])
                rmsnorm(n, 4 + lidx, lambda k: spf[:, k, :, PS:PS + cfg.dtok], "Oacc")
                for k in range(DC):
                    out_tickets.append(DMA(["Oacc"], [], out=poolT[o, k * 128:(k + 1) * 128, PS:].rearrange("p (s c) -> p s c", c=PS),
                                           in_=spf[:, k, :, cfg.dtok:cfg.dtok + PS]))
            for gi in range(4):
                for k in (2 * gi, 2 * gi + 1):
                    src, srcn = F, Fn
                    sh = 1
                    for step in range(gi + 1):
                        dst, dstn = (A_, An) if step % 2 == 0 else (B_, Bn)
                        lo = 2 * sh - 1
                        V("tensor_tensor", [srcn], [dstn], out=fv(dst, k, lo, W), in0=fv(src, k, lo, W),
                          in1=fv(src, k, lo - sh, W - sh), op=ALU.add)
                        src, srcn = dst, dstn
                        sh *= 2
                    w = 2 ** (gi + 1)
                    if (not sample) and t == 0:
                        V("tensor_tensor", [srcn, "icnt"], [srcn], out=fv(src, k, PS, W), in0=fv(src, k, PS, W),
                          in1=icnt[:, gi, :n], op=ALU.mult)
                        V("tensor_tensor", [srcn, Fn], ["qabsT"], out=pdiff[:, k, :n], in0=fv(src, k, PS, W),
                          in1=fv(F, k, PS, W), op=ALU.subtract)
                    else:
                        dview = pdiff[:, k, :n] if not sample else pdiff[:, k, :n].rearrange("p (s c) -> p s c", c=cfg.dtok)
                        V("scalar_tensor_tensor", [srcn, Fn], ["qabsT"], out=dview, in0=fv(src, k, PS, W), scalar=1.0 / w,
                          in1=fv(F, k, PS, W), op0=ALU.mult, op1=ALU.subtract)
            for gi in range(4):
                for cc in range(2):
                    DMA([], ["pw_st"], out=pw_st[:, cc, :], in_=pool_w[o, gi, cc * 128:(cc + 1) * 128, :])
                G("tensor_copy", ["pw_st"], ["pw_b"], out=pw_b[:, gi, :, :], in_=pw_st[:])
            for half in range(2):
                for dc4 in range(4):
                    dc = half * 4 + dc4
                    gi, dd = dc // 2, dc % 2
                    for cc in range(2):
                        T(["pw_b", "qabsT"], ["PA"], out=ps_y[:, dc4, :n], lhsT=pw_b[:, gi, cc, dd * 128:(dd + 1) * 128],
                          rhs=pdiff[:, 2 * gi + cc, :n], start=(cc == 0), stop=(cc == 1))
                for dc4 in range(4):
                    dc = half * 4 + dc4
                    V("scalar_tensor_tensor", ["PA", "x", "psc"], ["x"], out=x[:, dc, :n], in0=ps_y[:, dc4, :n],
                      scalar=psc[:, o, dc:dc + 1], in1=x[:, dc, :n], op0=ALU.mult, op1=ALU.add)
            barrier()

        dummy = sb("dummy_bar", [128, 1])
        DMA([], ["qng"], out=qng[:], in_=qng_d)
        DMA([], ["kvng"], out=kvng[:], in_=kvng_d)
        DMA([], ["cw"], out=cw[:], in_=cw_d)
        DMA([], ["cvec"], out=cvec[:], in_=cvec_d)
        DMA([], ["amask"], out=amask[:], in_=amask_d)
        DMA([], ["ident32"], out=ident32[:], in_=ident_d)
        V("tensor_copy", ["ident32"], ["identb"], out=identb[:], in_=ident32[:])
        V("memset", [], ["ones2"], ap=ones2[:], constant=1.0)
        V("memset", [], ["chalo"], ap=chalo[:], constant=0.0)
        V("memset", [], ["kr32"], ap=kr32[:], constant=0.0)
        V("memset", [], ["rq32"], ap=rq32[:], constant=0.0)

        DMA([], ["pt_i"], out=pt_i[:], in_=pt_d)
        DMA([], ["riota"], out=riota[:], in_=riota_d)
        DMA([], ["smask"], out=smask[:], in_=smask_d)
        V("tensor_copy", ["pt_i"], ["pt_f"], out=pt_f[:], in_=pt_i[:])
        V("tensor_scalar", ["pt_f", "riota"], ["pt_f"], out=pt_f[:], in0=pt_f[:], scalar1=float(PAGE), scalar2=riota[:, 0:1],
          op0=ALU.mult, op1=ALU.add)
        V("tensor_copy", ["pt_f"], ["idx32"], out=idx32[:], in_=pt_f[:])
        V("memset", [], ["rp48"], ap=rp48[:], constant=0.0)

        def GATHER(w, out, src, col):
            return pr.op("gpsimd", "indirect_dma_start", ["idx32"], w, dma=True, out=out, out_offset=None, in_=src,
                         in_offset=bass.IndirectOffsetOnAxis(ap=idx32[:, col:col + 1], axis=0))

        def barrier():
            V("memset", [], ["PHASE"], ap=dummy[:], constant=0.0)

        def TR(r, w, **kw):
            return pr.op("tensor", "transpose", r, w, **kw)

        def rms_small(src, srcn, nch, width, n):
            G("tensor_tensor", [srcn], ["sq"], out=sq[:, 0:nch, :n], in0=src[:, 0:nch, :n], in1=src[:, 0:nch, :n], op=ALU.mult)
            for k in range(nch):
                T(["ones", "sq"], ["PB"], out=PB[:, :n], lhsT=ones[:], rhs=sq[:, k, :n], start=(k == 0), stop=(k == nch - 1))
            V("tensor_scalar", ["PB"], ["rstd"], out=rstd[:, :n], in0=PB[:, :n], scalar1=1.0 / width, scalar2=EPS,
              op0=ALU.mult, op1=ALU.add)
            A("activation", ["rstd"], ["rstd"], out=rstd[:, :n], in_=rstd[:, :n], func=AF.Sqrt)
            V("reciprocal", ["rstd"], ["rstd"], out=rstd[:, :n], in_=rstd[:, :n])

        def rope_from(srcA_a, srcA_b, srcB_a, srcB_b, srcn, dstA, dstB, dstn, n):
            V("tensor_tensor", [srcn, "cs"], ["kt1"], out=kt1[0:16, :n], in0=srcA_a, in1=cs[0:16, 0, :n], op=ALU.mult)
            V("tensor_tensor", [srcn, "cs"], ["kt2"], out=kt2[0:16, :n], in0=srcA_b, in1=cs[0:16, 1, :n], op=ALU.mult)
            V("tensor_tensor", ["kt1", "kt2"], [dstn], out=dstA, in0=kt1[0:16, :n], in1=kt2[0:16, :n], op=ALU.subtract)
            V("tensor_tensor", [srcn, "cs"], ["kt1"], out=kt1[32:48, :n], in0=srcB_b, in1=cs[32:48, 0, :n], op=ALU.mult)
            V("tensor_tensor", [srcn, "cs"], ["kt2"], out=kt2[32:48, :n], in0=srcB_a, in1=cs[32:48, 1, :n], op=ALU.mult)
            V("tensor_tensor", ["kt1", "kt2"], [dstn], out=dstB, in0=kt1[32:48, :n], in1=kt2[32:48, :n], op=ALU.add)

        def even_layer(n, e, lidx, kind, t, off):
            sample = kind == "s"
            QBn = max(n // 128, 1)
            barrier()
            rmsnorm(n, 4 + lidx, lambda k: xn[:, k, :n], "xn")
            DMA([], ["cs"], out=cs[:, :, :n], in_=cs_d[:, :, off:off + n].rearrange("a p n -> p a n"))
            win = w_in[e].rearrange("(kc p) f -> p kc f", p=128)
            dests = {}
            for j in range(6):
                dests[128 * j] = (qlat[:, j, :n], "qlat")
            for j in range(2):
                dests[768 + 128 * j] = (kvlat[:, j, :n], "kvlat")
            for j in range(8):
                dests[1056 + 128 * j] = (ag[:, j, :n], "ag")
            for g0 in (0, 256, 512, 768, 1056, 1312, 1568, 1824):
                DMA([], ["wg_st"], out=wg_st[:], in_=win[:, :, g0:g0 + 256])
                G("tensor_copy", ["wg_st"], ["wg_b"], out=wg_b[:], in_=wg_st[:])
                for c in range(2):
                    dst, dn = dests[g0 + 128 * c]
                    for k in range(DC):
                        T(["wg_b", "xn"], ["PC"], out=PC[:, :n], lhsT=wg_b[:, k, c * 128:(c + 1) * 128], rhs=xn[:, k, :n],
                          start=(k == 0), stop=(k == DC - 1))
                    V("tensor_copy", ["PC"], [dn], out=dst, in_=PC[:, :n])
            DMA([], ["wu_st"], out=wu_st[:, :, 0:32], in_=win[:, :, 1024:1056])
            G("tensor_copy", ["wu_st"], ["wu_b"], out=wu_b[:, :, 0:32], in_=wu_st[:, :, 0:32])
            for (p0, half, c0) in ((0, 0, 0), (0, 1, 16), (32, 0, 0), (32, 1, 16)):
                for k in range(DC):
                    T(["wu_b", "xn"], ["PD"], out=PD[p0:p0 + 16, half * n:(half + 1) * n], lhsT=wu_b[:, k, c0:c0 + 16],
                      rhs=xn[:, k, :n], start=(k == 0), stop=(k == DC - 1))
            rope_from(PD[0:16, 0:n], PD[0:16, n:2 * n], PD[32:48, 0:n], PD[32:48, n:2 * n], "PD",
                      kr32[0:16, :n], kr32[32:48, :n], "kr32", n)
            out_tickets.append(DMA(["kr32"], [], out=krT[e, 0:16, off:off + n], in_=kr32[0:16, :n]))
            out_tickets.append(DMA(["kr32"], [], out=krT[e, 16:32, off:off + n], in_=kr32[32:48, :n]))
            V("tensor_copy", ["kr32"], ["krb"], out=krb[:, :n], in_=kr32[:, :n])
            rms_small(qlat, "qlat", 6, float(QL), n)
            for j in range(6):
                V("scalar_tensor_tensor", ["qlat", "qng", "rstd"], ["qn"], out=qn[:, j, :n], in0=qlat[:, j, :n],
                  scalar=qng[:, e, j:j + 1], in1=rstd[:, :n], op0=ALU.mult, op1=ALU.mult)
            rms_small(kvlat, "kvlat", 2, float(KVL), n)
            for j in range(2):
                V("scalar_tensor_tensor", ["kvlat", "kvng", "rstd"], ["ckv32"], out=ckv32[:, j, :n], in0=kvlat[:, j, :n],
                  scalar=kvng[:, e, j:j + 1], in1=rstd[:, :n], op0=ALU.mult, op1=ALU.mult)
                out_tickets.append(DMA(["ckv32"], [], out=latT[e, j * 128:(j + 1) * 128, off:off + n], in_=ckv32[:, j, :n]))
            V("tensor_copy", ["ckv32"], ["ckvb"], out=ckvb[:, :, :n], in_=ckv32[:, :, :n])
            wuq = w_uq[e].rearrange("(kc p) f -> p kc f", p=128)
            for g in range(4):
                DMA([], ["wg_st"], out=wg_st[:, 0:6, 0:192], in_=wuq[:, :, 192 * g:192 * g + 192])
                G("tensor_copy", ["wg_st"], ["wq_b"], out=wq_b[:], in_=wg_st[:, 0:6, 0:192])
                for hh in range(2):
                    h = 2 * g + hh
                    c0 = 96 * hh
                    for k in range(6):
                        T(["wq_b", "qn"], ["PC"], out=PC[0:64, :n], lhsT=wq_b[:, k, c0:c0 + 64], rhs=qn[:, k, :n],
                          start=(k == 0), stop=(k == 5))
                    V("tensor_copy", ["PC"], ["qnope"], out=qnope[0:64, h, :n], in_=PC[0:64, :n])
                    for (p0, half, cc0) in ((0, 0, 64), (0, 1, 80), (32, 0, 64), (32, 1, 80)):
                        for k in range(6):
                            T(["wq_b", "qn"], ["PD"], out=PD[p0:p0 + 16, half * n:(half + 1) * n],
                              lhsT=wq_b[:, k, c0 + cc0:c0 + cc0 + 16], rhs=qn[:, k, :n], start=(k == 0), stop=(k == 5))
                    rope_from(PD[0:16, 0:n], PD[0:16, n:2 * n], PD[32:48, 0:n], PD[32:48, n:2 * n], "PD",
                              rq32[0:16, h, :n], rq32[32:48, h, :n], "rq32", n)
            V("tensor_copy", ["rq32"], ["rqb"], out=rqb[:, :, :n], in_=rq32[:, :, :n])
            DMA([], ["wu_st"], out=wu_st[0:64, :, :], in_=w_ukT[e].rearrange("h d c -> d h c"))
            G("tensor_copy", ["wu_st"], ["wuk_b"], out=wuk_b[:], in_=wu_st[0:64, :, :])
            for h in range(NH):
                for cc in range(2):
                    T(["wuk_b", "qnope"], ["PC"], out=PC[:, :n], lhsT=wuk_b[0:64, h, cc * 128:(cc + 1) * 128],
                      rhs=qnope[0:64, h, :n], start=True, stop=True)
                    V("tensor_copy", ["PC"], ["qabsT"], out=qabsT[:, cc, h, :n], in_=PC[:, :n])
            A("activation", ["ag"], ["ag"], out=ag[:, 4:8, :n], in_=ag[:, 4:8, :n], func=AF.Sigmoid)
            if not sample:
                G("tensor_copy", ["chalo"], ["uT"], out=uT[:, :, 0:HW_], in_=chalo[:, e, :, :])
                V("tensor_tensor", ["ag"], ["uT"], out=uT[:, :, HW_:HW_ + n], in0=ag[:, 0:4, :n], in1=ag[:, 4:8, :n], op=ALU.mult)
                G("tensor_copy", ["uT"], ["chalo"], out=chalo[:, e, :, :], in_=uT[:, :, n:n + HW_])
                if t == cfg.ntiles - 1:
                    for j in range(4):
                        out_tickets.append(DMA(["uT"], [], out=convT[e, j * 128:(j + 1) * 128, 0:HW_], in_=uT[:, j, n:n + HW_]))
                uview = lambda j, tap: uT[:, j, tap:tap + n]
                cview = lambda j: cacc[:, j, :n]
            else:
                LS = HW_ + cfg.dtok
                su = uT[:, :, 0:cfg.dseq * LS].rearrange("p j (s c) -> p j s c", c=LS)
                for j in range(4):
                    DMA([], ["uT"], out=su[:, j, :, 0:HW_], in_=conv_hist[e, j * 128:(j + 1) * 128, :, :])
                    V("tensor_tensor", ["ag"], ["uT"], out=su[:, j, :, HW_:LS],
                      in0=ag[:, j, :n].rearrange("p (s c) -> p s c", c=cfg.dtok),
                      in1=ag[:, 4 + j, :n].rearrange("p (s c) -> p s c", c=cfg.dtok), op=ALU.mult)
                    out_tickets.append(DMA(["uT"], [], out=convT[e, j * 128:(j + 1) * 128, HW_:].rearrange("p (s c) -> p s c", c=HW_),
                                           in_=su[:, j, :, cfg.dtok:LS]))
                uview = lambda j, tap: su[:, j, :, tap:tap + cfg.dtok]
                cview = lambda j: cacc[:, j, :n].rearrange("p (s c) -> p s c", c=cfg.dtok)
            for j in range(4):
                V("tensor_scalar", ["uT", "cw", "cvec"], ["cacc"], out=cview(j), in0=uview(j, 0), scalar1=cw[:, e, j, 0:1],
                  scalar2=cvec[:, 0, e, j:j + 1], op0=ALU.mult, op1=ALU.add)
                for tap in range(1, CW):
                    V("scalar_tensor_tensor", ["uT", "cw", "cacc"], ["cacc"], out=cview(j), in0=uview(j, tap),
                      scalar=cw[:, e, j, tap:tap + 1], in1=cview(j), op0=ALU.mult, op1=ALU.add)
            for j in range(4):
                T(["ones", "cacc"], ["PB"], out=PB[:, :n], lhsT=ones[:], rhs=cacc[:, j, :n], start=(j == 0), stop=(j == 3))
            V("tensor_scalar", ["PB"], ["cmean"], out=cmean[:, :n], in0=PB[:, :n], scalar1=1.0 / CCH, scalar2=None, op0=ALU.mult)
            for j in range(4):
                V("tensor_tensor", ["cacc", "cmean"], ["cacc"], out=cacc[:, j, :n], in0=cacc[:, j, :n], in1=cmean[:, :n],
                  op=ALU.subtract)
            rms_small(cacc, "cacc", 4, float(CCH), n)
            for j in range(4):
                V("tensor_tensor", ["cacc", "rstd"], ["cacc"], out=cacc[:, j, :n], in0=cacc[:, j, :n], in1=rstd[:, :n], op=ALU.mult)
                A("activation", ["cacc", "cvec"], ["cv"], out=cv[:, j, :n], in_=cacc[:, j, :n], func=AF.Silu,
                  scale=cvec[:, 1, e, j:j + 1], bias=cvec[:, 2, e, j:j + 1])
            if not sample:
                DMA(["ckvb"], ["kcT"], out=kcT[e][:, :, off:off + n], in_=ckvb[:, :, :n])
                DMA(["krb"], ["kcR"], out=kcR[e][:, off:off + n], in_=krb[:, :n])
                for blk in range(QBn):
                    for cc in range(2):
                        TR(["ckvb", "identb"], ["PT_"], out=PT_[:, cc * 128:(cc + 1) * 128],
                           in_=ckvb[:, cc, blk * 128:(blk + 1) * 128], identity=identb[:])
                    V("tensor_copy", ["PT_"], ["kaug"], out=kaug[:, blk, :], in_=PT_[:, 0:KVL])
                    DMA(["kaug"], ["kcA"], out=kcA[e, off // 128 + blk], in_=kaug[:, blk, :])
                nkb = (off + n) // 128
                for j in range(nkb):
                    DMA(["kcT"], ["kbT"], out=kbT[:], in_=kcT[e][:, :, j * 128:(j + 1) * 128])
                    DMA(["kcR"], ["kbR"], out=kbR[:], in_=kcR[e][:, j * 128:(j + 1) * 128])
                    DMA(["kcA"], ["kbA"], out=kbA[:], in_=kcA[e, j])
                    for h in range(NH):
                        T(["kbT", "qabsT"], ["PC"], out=PC[:, :n], lhsT=kbT[:, 0, :], rhs=qabsT[:, 0, h, :n], start=True, stop=False)
                        T(["kbT", "qabsT"], ["PC"], out=PC[:, :n], lhsT=kbT[:, 1, :], rhs=qabsT[:, 1, h, :n], start=False, stop=False)
                        T(["kbR", "rqb"], ["PC"], out=PC[:, :n], lhsT=kbR[:, :], rhs=rqb[:, h, :n], start=False, stop=True)
                        A("activation", ["PC"], ["PTs"], out=PTs[:, :n], in_=PC[:, :n], func=AF.Exp, scale=SCALE)
                        jj = j - off // 128
                        if jj >= 0:
                            V("tensor_tensor", ["PTs", "amask"], ["PTs"], out=PTs[:, :n], in0=PTs[:, :n], in1=amask[:, jj, :n],
                              op=ALU.mult)
                        for qb in range(QBn):
                            T(["PTs", "kbA"], ["PA"], out=PA[:, qb, :], lhsT=PTs[:, qb * 128:(qb + 1) * 128], rhs=kbA[:, :],
                              start=True, stop=True)
                            T(["PTs", "ones2"], ["PB"], out=PB[:, 2 * qb:2 * qb + 2], lhsT=PTs[:, qb * 128:(qb + 1) * 128],
                              rhs=ones2[:], start=True, stop=True)
                        lview = PB[:, 0:2 * QBn].rearrange("p (i two) -> p i two", two=2)[:, :, 0]
                        if j == 0:
                            V("tensor_copy", ["PA"], ["Oacc"], out=Oacc[:, h * QBn:(h + 1) * QBn, :], in_=PA[:, 0:QBn, :])
                            V("tensor_copy", ["PB"], ["Lacc"], out=Lacc[:, h * QBn:(h + 1) * QBn], in_=lview)
                        else:
                            V("tensor_tensor", ["PA", "Oacc"], ["Oacc"], out=Oacc[:, h * QBn:(h + 1) * QBn, :],
                              in0=PA[:, 0:QBn, :], in1=Oacc[:, h * QBn:(h + 1) * QBn, :], op=ALU.add)
                            V("tensor_tensor", ["PB", "Lacc"], ["Lacc"], out=Lacc[:, h * QBn:(h + 1) * QBn],
                              in0=lview, in1=Lacc[:, h * QBn:(h + 1) * QBn], op=ALU.add)
                V("reciprocal", ["Lacc"], ["rec"], out=rec[:, 0:NH * QBn], in_=Lacc[:, 0:NH * QBn])
                for h in range(NH):
                    for qb in range(QBn):
                        idx = h * QBn + qb
                        V("tensor_scalar", ["Oacc", "rec"], ["Onb"], out=Onb[:], in0=Oacc[:, idx, :], scalar1=rec[:, idx:idx + 1],
                          scalar2=None, op0=ALU.mult)
                        for cc in range(2):
                            TR(["Onb", "identb"], ["PT_"], out=PT_[:, cc * 128:(cc + 1) * 128],
                               in_=Onb[:, cc * 128:(cc + 1) * 128], identity=identb[:])
                        V("tensor_copy", ["PT_"], ["OT"], out=OT[:, :, h, qb * 128:(qb + 1) * 128],
                          in_=PT_[:, 0:256].rearrange("p (c q) -> p c q", c=2))
            else:
                dtk = cfg.dtok
                NQ = NH * dtk
                for sq_ in range(cfg.dseq):
                    t0 = sq_ * dtk
                    qa = lambda cc: qabsT[:, cc, :, t0:t0 + dtk]
                    for pg in range(cfg.npages + 1):
                        new = pg == cfg.npages
                        if not new:
                            col = sq_ * cfg.npages + pg
                            GATHER(["lp32"], lp32[:, :], cache_lat[e][:, :], col)
                            GATHER(["rp32"], rp32[:, :], cache_kr[e][:, :], col)
                            V("tensor_copy", ["lp32"], ["lpb"], out=lpb[:], in_=lp32[:])
                            V("tensor_copy", ["rp32"], ["rp48"], out=rp48[:, 0:16], in_=rp32[:, 0:16])
                            V("tensor_copy", ["rp32"], ["rp48"], out=rp48[:, 32:48], in_=rp32[:, 16:32])
                            for cc in range(2):
                                TR(["lpb", "identb"], ["PT_"], out=PT_[:, cc * 128:(cc + 1) * 128], in_=lpb[:, cc * 128:(cc + 1) * 128],
                                   identity=identb[:])
                            TR(["rp48", "identb"], ["PT_"], out=PT_[0:48, 256:384], in_=rp48[:, :], identity=identb[:])
                            V("tensor_copy", ["PT_"], ["kbT"], out=kbT[:], in_=PT_[:, 0:256].rearrange("p (c q) -> p c q", c=2))
                            V("tensor_copy", ["PT_"], ["kbR"], out=kbR[:], in_=PT_[0:48, 256:384])
                            KP = 128
                            lk = lambda cc: kbT[:, cc, :]
                            lr = kbR[:, :]
                            vrhs, vn = lpb[:, :], "lpb"
                        else:
                            KP = dtk
                            lk = lambda cc: ckvb[:, cc, t0:t0 + dtk]
                            lr = krb[:, t0:t0 + dtk]
                            for cc in range(2):
                                TR(["ckvb", "identb"], ["PT_"], out=PT_[0:dtk, cc * 128:(cc + 1) * 128], in_=ckvb[:, cc, t0:t0 + dtk],
                                   identity=identb[:])
                            V("tensor_copy", ["PT_"], ["knew"], out=knew[0:dtk, :], in_=PT_[0:dtk, 0:256])
                            vrhs, vn = knew[0:dtk, :], "knew"
                        T(["kbT", "ckvb", "qabsT"], ["PC"], out=PC[0:KP, 0:NQ], lhsT=lk(0), rhs=qa(0), start=True, stop=False)
                        T(["kbT", "ckvb", "qabsT"], ["PC"], out=PC[0:KP, 0:NQ], lhsT=lk(1), rhs=qa(1), start=False, stop=False)
                        T(["kbR", "krb", "rqb"], ["PC"], out=PC[0:KP, 0:NQ], lhsT=lr, rhs=rqb[:, :, t0:t0 + dtk], start=False, stop=True)
                        A("activation", ["PC"], ["PTs"], out=PTs[0:KP, 0:NQ], in_=PC[0:KP, 0:NQ], func=AF.Exp, scale=SCALE)
                        if new:
                            V("tensor_tensor", ["PTs", "smask"], ["PTs"], out=PTs[0:KP, 0:NQ], in0=PTs[0:KP, 0:NQ], in1=smask[0:KP, :],
                              op=ALU.mult)
                        T(["PTs", vn], ["PA"], out=PA[0:NQ, 0, :], lhsT=PTs[0:KP, 0:NQ], rhs=vrhs, start=True, stop=True)
                        T(["PTs", "ones2"], ["PB"], out=PB[0:NQ, 0:2], lhsT=PTs[0:KP, 0:NQ], rhs=ones2[0:KP, :], start=True, stop=True)
                        if pg == 0:
                            V("tensor_copy", ["PA"], ["Oacc"], out=Oacc[0:NQ, 0, :], in_=PA[0:NQ, 0, :])
                            V("tensor_copy", ["PB"], ["Lacc"], out=Lacc[0:NQ, 0:1], in_=PB[0:NQ, 0:1])
                        else:
                            V("tensor_tensor", ["PA", "Oacc"], ["Oacc"], out=Oacc[0:NQ, 0, :], in0=PA[0:NQ, 0, :], in1=Oacc[0:NQ, 0, :],
                              op=ALU.add)
                            V("tensor_tensor", ["PB", "Lacc"], ["Lacc"], out=Lacc[0:NQ, 0:1], in0=PB[0:NQ, 0:1], in1=Lacc[0:NQ, 0:1],
                              op=ALU.add)
                    V("reciprocal", ["Lacc"], ["rec"], out=rec[0:NQ, 0:1], in_=Lacc[0:NQ, 0:1])
                    V("tensor_scalar", ["Oacc", "rec"], ["Onb"], out=Onb[0:NQ, :], in0=Oacc[0:NQ, 0, :], scalar1=rec[0:NQ, 0:1],
                      scalar2=None, op0=ALU.mult)
                    for cc in range(2):
                        TR(["Onb", "identb"], ["PT_"], out=PT_[:, cc * NQ:(cc + 1) * NQ], in_=Onb[0:NQ, cc * 128:(cc + 1) * 128],
                           identity=identb[0:NQ, 0:NQ])
                    for cc in range(2):
                        V("tensor_copy", ["PT_"], ["OT"], out=OT[:, cc, :, t0:t0 + dtk],
                          in_=PT_[:, cc * NQ:(cc + 1) * NQ].rearrange("p (h t) -> p h t", t=dtk))
            DMA([], ["wd_st"], out=wd_st[:, :, 0:NH * VD], in_=w_uv[e].rearrange("(c p) f -> p c f", p=128))
            G("tensor_copy", ["wd_st"], ["wuv_b"], out=wuv_b[:], in_=wd_st[:, :, 0:NH * VD])
            for h in range(NH):
                for cc in range(2):
                    T(["wuv_b", "OT"], ["PC"], out=PC[0:64, :n], lhsT=wuv_b[:, cc, h * VD:(h + 1) * VD], rhs=OT[:, cc, h, :n],
                      start=(cc == 0), stop=(cc == 1))
                V("tensor_copy", ["PC"], ["attT"], out=attT[0:64, h, :n], in_=PC[0:64, :n])
            for dc in range(DC):
                DMA([], ["wg_st"], out=wg_st[0:64, :, 0:128],
                    in_=w_out[e, 0:NH * VD, dc * 128:(dc + 1) * 128].rearrange("(h v) d -> v h d", v=VD))
                DMA([], ["wu_st"], out=wu_st[:, 0:4, 0:128],
                    in_=w_out[e, NH * VD:D, dc * 128:(dc + 1) * 128].rearrange("(j p) d -> p j d", p=128))
                G("tensor_copy", ["wg_st"], ["woa_b"], out=woa_b[:], in_=wg_st[0:64, :, 0:128])
                G("tensor_copy", ["wu_st"], ["woc_b"], out=woc_b[:], in_=wu_st[:, 0:4, 0:128])
                for h in range(NH):
                    T(["woa_b", "attT"], ["PC"], out=PC[:, :n], lhsT=woa_b[0:64, h, :], rhs=attT[0:64, h, :n], start=(h == 0), stop=False)
                for j in range(4):
                    T(["woc_b", "cv"], ["PC"], out=PC[:, :n], lhsT=woc_b[:, j, :], rhs=cv[:, j, :n], start=False, stop=(j == 3))
                V("tensor_tensor", ["PC", "x"], ["x"], out=x[:, dc, :n], in0=PC[:, :n], in1=x[:, dc, :n], op=ALU.add)
            barrier()

        tiles = [("p", t, NT, t * NT) for t in range(cfg.ntiles)] + [("s", 0, NS, SEQ)]
        for kind, t, n, off in tiles:
            for k in range(DC):
                DMA([], ["x"], out=x[:, k, :n], in_=xT[k * 128:(k + 1) * 128, off:off + n])
            for l in range(n_layers):
                if "f" in cfg.parts:
                    ffn(n, 2 * l, l)
                if l % 2 == 1 and "p" in cfg.parts:
                    pool_layer(n, l // 2, l, kind, t)
                if l % 2 == 0 and "a" in cfg.parts:
                    even_layer(n, l // 2, l, kind, t, off)
                if "f" in cfg.parts:
                    ffn(n, 2 * l + 1, 8 + l)
            rmsnorm(n, 12, lambda k: sq[:, k, :n], "sq")
            for k in range(DC):
                out_tickets.append(DMA(["sq"], [], out=yT[k * 128:(k + 1) * 128, off:off + n], in_=sq[:, k, :n]))

        pr.finish("sync", out_tickets)
        with nc.Block() as block:
            pr.emit(block)
    return nc


def prep_core(inp, cfg, c):
    b = c % 2
    s0, s1 = c * cfg.dseq, (c + 1) * cfg.dseq
    f = lambda a: np.ascontiguousarray(np.asarray(a, dtype=np.float32))
    m = {}
    m["xT"] = f(np.concatenate([inp["x_prompt"][b].T, inp["x_sample"][s0:s1].reshape(-1, D).T], axis=1))
    nr = np.concatenate([inp["ffn1_norm"], inp["mix_norm"], inp["ffn2_norm"], inp["final_norm"][None]], 0)
    m["norms"] = f(nr.reshape(13, DC, 128).transpose(2, 0, 1))
    il = lambda a1, a2: np.stack([a1, a2], 1).reshape((2 * DEPTH,) + a1.shape[1:])
    m["wg"] = f(il(inp["ffn1_w_gate"], inp["ffn2_w_gate"]))
    m["wu"] = f(il(inp["ffn1_w_up"], inp["ffn2_w_up"]))
    m["wd"] = f(il(inp["ffn1_w_down"], inp["ffn2_w_down"]))
    m["pool_w"] = f(inp["pool_w"])
    m["pool_sc"] = f(inp["pool_scale"].reshape(2, DC, 128).transpose(2, 0, 1))
    m["pool_hist"] = f(inp["state_pool"][:, s0:s1].transpose(0, 3, 1, 2))
    pos = np.arange(cfg.nt) + 1
    ic = np.stack([1.0 / np.minimum(pos, w) for w in (2, 4, 8, 16)], 0)
    m["invcnt"] = f(np.broadcast_to(ic[None], (128, 4, cfg.nt)))
    m["w_in"] = f(inp["w_in"])
    m["qng"] = f(inp["q_norm"].reshape(2, 6, 128).transpose(2, 0, 1))
    m["kvng"] = f(inp["kv_norm"].reshape(2, 2, 128).transpose(2, 0, 1))
    m["w_uq"] = f(inp["w_uq"].reshape(2, QL, NH * (NOPE + ROPE)))
    m["w_ukT"] = f(inp["w_uk"].transpose(0, 2, 3, 1))
    m["w_uv"] = f(inp["w_uv"].reshape(2, KVL, NH * VD))
    m["cw"] = f(inp["conv_w"].transpose(0, 2, 1).reshape(2, 4, 128, CW).transpose(2, 0, 1, 3))
    cvv = np.stack([inp["conv_b"], inp["conv_ln_g"], inp["conv_ln_b"]], 0)
    m["cvec"] = f(cvv.reshape(3, 2, 4, 128).transpose(3, 0, 1, 2))
    m["w_out"] = f(inp["w_out"])
    past = cfg.npages * PAGE
    pos = np.concatenate([np.arange(cfg.seq, dtype=np.float32),
                          np.tile(past + np.arange(cfg.dtok, dtype=np.float32), cfg.dseq)]).astype(np.float32)
    inv = (np.float32(10000.0) ** (-np.arange(0, ROPE, 2, dtype=np.float32) / np.float32(ROPE))).astype(np.float32)
    ang = (pos[None, :] * inv[:, None]).astype(np.float32)
    cs = np.zeros((2, 48, pos.shape[0]), np.float32)
    for a, fn in enumerate((np.cos, np.sin)):
        cs[a, 0:16] = fn(ang)
        cs[a, 32:48] = fn(ang)
    m["cossin"] = cs
    m["conv_hist"] = f(inp["state_conv"][:, s0:s1].transpose(0, 3, 1, 2))
    qb = cfg.nt // 128
    key = np.arange(128)[:, None, None] + 128 * np.arange(qb)[None, :, None]
    qq = np.arange(cfg.nt)[None, None, :]
    m["amask"] = f((key <= qq).astype(np.float32))
    m["ident"] = np.eye(128, dtype=np.float32)
    for i in range(2):
        m[f"cache_lat{i}"] = f(inp["cache_latent"][i]).reshape(-1, KVL)
        m[f"cache_kr{i}"] = f(inp["cache_krope"][i]).reshape(-1, ROPE)
    pt = np.ascontiguousarray(np.asarray(inp["page_table"])[s0:s1].reshape(1, -1).astype(np.int32))
    m["pt_rep"] = np.ascontiguousarray(np.broadcast_to(pt, (128, pt.shape[1])))
    m["rowiota"] = np.arange(128, dtype=np.float32).reshape(128, 1)
    tk = np.arange(cfg.dtok)
    sm = (tk[:, None, None] <= tk[None, None, :]).astype(np.float32)
    m["smask"] = f(np.broadcast_to(sm, (cfg.dtok, NH, cfg.dtok)).reshape(cfg.dtok, NH * cfg.dtok))
    return m


def assemble(res, cfg, B, DB):
    SEQ, ds, dt = cfg.seq, cfg.dseq, cfg.dtok
    y_p = np.zeros((B, SEQ, D), np.float32)
    y_s = np.zeros((DB, dt, D), np.float32)
    pool_p = np.zeros((2, B, PS, D), np.float32)
    pool_s = np.zeros((2, DB, PS, D), np.float32)
    for c, r in enumerate(res):
        s0, s1 = c * ds, (c + 1) * ds
        if c < B:
            y_p[c] = r["yT"][:, :SEQ].T
            for o in range(2):
                pool_p[o, c] = r["poolT"][o][:, :PS].T
        y_s[s0:s1] = r["yT"][:, SEQ:].T.reshape(ds, dt, D)
        for o in range(2):
            pool_s[o, s0:s1] = r["poolT"][o][:, PS:].reshape(D, ds, PS).transpose(1, 2, 0)
    lat_p = np.zeros((2, B, SEQ, KVL), np.float32)
    kr_p = np.zeros((2, B, SEQ, ROPE), np.float32)
    lat_s = np.zeros((2, DB, dt, KVL), np.float32)
    kr_s = np.zeros((2, DB, dt, ROPE), np.float32)
    conv_p = np.zeros((2, B, CW - 1, CCH), np.float32)
    conv_s = np.zeros((2, DB, CW - 1, CCH), np.float32)
    H = CW - 1
    for c, r in enumerate(res):
        s0, s1 = c * ds, (c + 1) * ds
        for e in range(2):
            if c < B:
                lat_p[e, c] = r["latT"][e][:, :SEQ].T
                kr_p[e, c] = r["krT"][e][:, :SEQ].T
                conv_p[e, c] = r["convT"][e][:, :H].T
            lat_s[e, s0:s1] = r["latT"][e][:, SEQ:].T.reshape(ds, dt, KVL)
            kr_s[e, s0:s1] = r["krT"][e][:, SEQ:].T.reshape(ds, dt, ROPE)
            conv_s[e, s0:s1] = r["convT"][e][:, H:].reshape(CCH, ds, H).transpose(1, 2, 0)
    return dict(y_p=y_p, y_s=y_s, pool_p=pool_p, pool_s=pool_s, lat_p=lat_p, kr_p=kr_p, lat_s=lat_s, kr_s=kr_s,
                conv_p=conv_p, conv_s=conv_s)


def run_cfg(inputs, cfg, B, DB):
    inp = {k: np.asarray(v) for k, v in inputs.items()}
    nc = build(cfg)
    in_maps = [prep_core(inp, cfg, c) for c in range(cfg.ncores)]
    res = run_bass_kernel_spmd(nc, in_maps, core_ids=list(range(cfg.ncores)))
    o = assemble(res.results, cfg, B, DB)
    return (o["y_p"], o["y_s"], o["lat_p"], o["kr_p"], o["lat_s"], o["kr_s"], o["conv_p"], o["conv_s"], o["pool_p"], o["pool_s"])


def kernel(**inputs):
    cfg = Cfg(seq=8192, nt=256, dseq=16, dtok=4, npages=64, nphys=int(np.asarray(inputs["cache_latent"]).shape[1]), ncores=8)
    cfg.parts = "fpa"
    return run_cfg(inputs, cfg, 2, 128)
```

```python
from contextlib import ExitStack
import numpy as np
import concourse.bass as bass
import concourse.mybir as mybir
from concourse.bass_utils import run_bass_kernel_spmd

F32 = mybir.dt.float32
BF16 = mybir.dt.bfloat16
I32 = mybir.dt.int32
ALU = mybir.AluOpType
AF = mybir.ActivationFunctionType

D = 1024
DC = 8
DFF = 2816
FC = 22
QL = 768
KVL = 256
ROPE = 32
NH = 8
NOPE = 64
VD = 64
CCH = 512
CW = 31
INW = 2080
PS = 15
EPS = 1e-6
DEPTH = 4
PAGE = 128
SCALE = float((NOPE + ROPE) ** -0.5)


class Cfg:
    def __init__(self, seq=8192, nt=512, dseq=16, dtok=4, npages=64, nphys=10240, ncores=8):
        self.seq, self.nt, self.dseq, self.dtok, self.npages, self.nphys = seq, nt, dseq, dtok, npages, nphys
        self.ncores = ncores
        self.ntiles = seq // nt
        self.ns = dseq * dtok
        self.parts = 'fpa'


class Prog:
    def __init__(self, nc, stack):
        self.nc, self.stack = nc, stack
        self.engs = ["sync", "scalar", "gpsimd", "vector", "tensor"]
        self.stream = {e: [] for e in self.engs}
        self.cmp = {}
        self.dmapool = {}
        self.dmaidx = {}
        self.seen = {e: {} for e in self.engs}
        self.bufs = {}
        self.nsem = 0
        self.last = {}
        self.owner = {}

    def _sem(self):
        self.nsem += 1
        return self.stack.enter_context(self.nc.semaphore(f"sem{self.nsem}"))

    def op(self, e, meth, reads=(), writes=(), dma=False, **kw):
        fn = (meth, kw)
        waits = {}
        reads = list(reads)
        writes = list(writes)
        if "PHASE" not in writes:
            reads.append("PHASE")

        def need(t):
            if t is None:
                return
            s, v = t
            k = id(s)
            if e == "tensor" and self.owner.get(k) == "tensor":
                return
            if self.seen[e].get(k, 0) >= v:
                return
            if k not in waits or waits[k][1] < v:
                waits[k] = (s, v)

        if dma:
            pool = self.dmapool.setdefault(e, [])
            i = self.dmaidx.get(e, 0)
            self.dmaidx[e] = i + 1
            slot = i % 8
            if len(pool) <= slot:
                pool.append([self._sem(), 0])
            if pool[slot][1] > 30000:
                need((pool[slot][0], pool[slot][1]))
                cur = pool[slot] = [self._sem(), 0]
            else:
                cur = pool[slot]
                need((cur[0], cur[1]) if cur[1] else None)
            inc = 16
        else:
            if e not in self.cmp or self.cmp[e][1] > 30000:
                self.cmp[e] = [self._sem(), 0]
                self.owner[id(self.cmp[e][0])] = e
            cur = self.cmp[e]
            inc = 1
        for b in reads:
            need(self.bufs.setdefault(b, {"w": None, "r": {}})["w"])
        for b in writes:
            d = self.bufs.setdefault(b, {"w": None, "r": {}})
            need(d["w"])
            for t in d["r"].values():
                need(t)
        for k, (s, v) in waits.items():
            self.seen[e][k] = v
        cur[1] += inc
        tk = (cur[0], cur[1])
        self.stream[e].append((list(waits.values()), fn, tk[0], inc))
        for b in reads:
            self.bufs[b]["r"][id(tk[0])] = tk
        for b in writes:
            self.bufs[b]["w"] = tk
            self.bufs[b]["r"] = {}
        self.last[e] = tk
        return tk

    def finish(self, e, tickets):
        best = {}
        for s_, v in tickets:
            if id(s_) not in best or best[id(s_)][1] < v:
                best[id(s_)] = (s_, v)
        self.stream[e].append((list(best.values()), None, None, 0))

    def emit(self, block):
        def mk(e):
            def body(eng):
                for waits, fn, sem, inc in self.stream[e]:
                    for s, v in waits:
                        eng.wait_ge(s, v)
                    if fn is not None:
                        getattr(eng, fn[0])(**fn[1]).then_inc(sem, inc)
            return body
        for e in self.engs:
            if self.stream[e]:
                getattr(block, e)(mk(e))


def build(cfg, n_layers=DEPTH):
    nc = bass.Bass("TRN2", target_bir_lowering=False)
    try:
        nc.allow_low_precision("bf16 matmul operands with fp32 accumulation")
    except Exception:
        pass
    NT, SEQ, NS = cfg.nt, cfg.seq, cfg.ns
    NTOK = SEQ + NS

    def din(name, shape, dt=F32):
        return nc.dram_tensor(name, list(shape), dt, kind="ExternalInput").ap()

    def dout(name, shape, dt=F32):
        return nc.dram_tensor(name, list(shape), dt, kind="ExternalOutput").ap()

    xT = din("xT", [D, NTOK])
    norms = din("norms", [128, 13, DC])
    wg = din("wg", [2 * DEPTH, D, DFF])
    wu = din("wu", [2 * DEPTH, D, DFF])
    wd = din("wd", [2 * DEPTH, DFF, D])
    pool_w = din("pool_w", [2, 4, 256, 256])
    pool_sc = din("pool_sc", [128, 2, DC])
    pool_hist = din("pool_hist", [2, D, cfg.dseq, PS])
    invcnt = din("invcnt", [128, 4, NT])
    w_in = din("w_in", [2, D, INW])
    qng_d = din("qng", [128, 2, 6])
    kvng_d = din("kvng", [128, 2, 2])
    w_uq = din("w_uq", [2, QL, 768])
    w_ukT = din("w_ukT", [2, NH, NOPE, KVL])
    w_uv = din("w_uv", [2, KVL, NH * VD])
    cw_d = din("cw", [128, 2, 4, CW])
    cvec_d = din("cvec", [128, 3, 2, 4])
    w_out = din("w_out", [2, D, D])
    cs_d = din("cossin", [2, 48, NTOK])
    conv_hist = din("conv_hist", [2, CCH, cfg.dseq, CW - 1])
    amask_d = din("amask", [128, NT // 128, NT])
    ident_d = din("ident", [128, 128])
    cache_lat = [din(f"cache_lat{i}", [cfg.nphys * PAGE, KVL]) for i in range(2)]
    cache_kr = [din(f"cache_kr{i}", [cfg.nphys * PAGE, ROPE]) for i in range(2)]
    pt_d = din("pt_rep", [128, cfg.dseq * cfg.npages], I32)
    riota_d = din("rowiota", [128, 1])
    smask_d = din("smask", [4, NH * 4])
    latT = dout("latT", [2, KVL, NTOK])
    krT = dout("krT", [2, ROPE, NTOK])
    convT = dout("convT", [2, CCH, (CW - 1) * (1 + cfg.dseq)])
    kcT = nc.dram_tensor("kcT", [2, 128, 2, SEQ], BF16, kind="Internal").ap()
    kcR = nc.dram_tensor("kcR", [2, 48, SEQ], BF16, kind="Internal").ap()
    kcA = nc.dram_tensor("kcA", [2, SEQ // 128, 128, KVL], BF16, kind="Internal").ap()
    yT = dout("yT", [D, NTOK])
    poolT = dout("poolT", [2, D, PS + cfg.dseq * PS])

    stack = ExitStack()
    with stack:
        pr = Prog(nc, stack)

        def sb(name, shape, dt=F32):
            return stack.enter_context(nc.sbuf_tensor(name, list(shape), dt))

        def ps(name, shape, dt=F32):
            return stack.enter_context(nc.psum_tensor(name, list(shape), dt))

        x = sb("x", [128, DC, NT])
        xn = sb("xn", [128, DC, NT], BF16)
        sq = sb("sq", [128, DC, NT])
        rstd = sb("rstd", [128, NT])
        ones = sb("ones", [128, 128])
        nrm = sb("nrm", [128, 13, DC])
        wg_st = sb("wg_st", [128, DC, 256])
        wu_st = sb("wu_st", [128, DC, 256])
        wd_st = sb("wd_st", [128, 2, D])
        wg_b = sb("wg_b", [128, DC, 256], BF16)
        wu_b = sb("wu_b", [128, DC, 256], BF16)
        wd_b = sb("wd_b", [128, 2, D], BF16)
        hs = sb("hs", [128, 2, NT])
        hb = sb("hb", [128, 2, NT], BF16)
        phalo = sb("phalo", [128, 2, DC, PS])
        psc = sb("psc", [128, 2, DC])
        icnt = sb("icnt", [128, 4, NT])
        pw_st = sb("pw_st", [128, 2, 256])
        pw_b = sb("pw_b", [128, 4, 2, 256], BF16)

        PA = ps("PA", [128, 8, 256])
        PB = ps("PB", [128, 512])
        PC = ps("PC", [128, 512])
        PD = ps("PD", [128, 512])
        PT_ = ps("PT_", [128, 512], BF16)
        ps_r, ps_g, ps_u, ps_y = PB, PC, PD, PA

        QB = NT // 128
        HW_ = CW - 1
        qlat = sb("qlat", [128, 6, NT])
        qn = sb("qn", [128, 6, NT], BF16)
        kvlat = sb("kvlat", [128, 2, NT])
        ckv32 = sb("ckv32", [128, 2, NT])
        ckvb = sb("ckvb", [128, 2, NT], BF16)
        kab = sb("kab", [48, 2, NT])
        kt1 = sb("kt1", [48, NT])
        kt2 = sb("kt2", [48, NT])
        kr32 = sb("kr32", [48, NT])
        krb = sb("krb", [48, NT], BF16)
        cs = sb("cs", [48, 2, NT])
        PW = max(PS + NT, cfg.dseq * (PS + cfg.dtok))
        ag = sb("ag", [128, 8, max(NT, PW)])
        uT = sb("uT", [128, 4, max(HW_ + NT, cfg.dseq * (HW_ + cfg.dtok), 2 * PW)])
        chalo = sb("chalo", [128, 2, 4, HW_])
        cacc = sb("cacc", [128, 4, NT])
        cmean = sb("cmean", [128, NT])
        cv = sb("cv", [128, 4, NT], BF16)
        qnope = sb("qnope", [64, NH, NT], BF16)
        qabsT = sb("qabsT", [128, 2, NH, NT], BF16)
        rq32 = sb("rq32", [48, NH, NT])
        rqb = sb("rqb", [48, NH, NT], BF16)
        wq_b = sb("wq_b", [128, 6, 192], BF16)
        wuk_b = sb("wuk_b", [64, NH, KVL], BF16)
        wuv_b = sb("wuv_b", [128, 2, NH * VD], BF16)
        woa_b = sb("woa_b", [64, NH, 128], BF16)
        woc_b = sb("woc_b", [128, 4, 128], BF16)
        qng = sb("qng_s", [128, 2, 6])
        kvng = sb("kvng_s", [128, 2, 2])
        cw = sb("cw_s", [128, 2, 4, CW])
        cvec = sb("cvec_s", [128, 3, 2, 4])
        amask = sb("amask_s", [128, QB, NT])
        ident32 = sb("ident32", [128, 128])
        identb = sb("identb", [128, 128], BF16)
        ones2 = sb("ones2", [128, 2], BF16)
        kaug = sb("kaug", [128, QB, KVL], BF16)
        kbT = sb("kbT", [128, 2, 128], BF16)
        kbR = sb("kbR", [48, 128], BF16)
        kbA = sb("kbA", [128, KVL], BF16)
        PTs = sb("PTs", [128, 2 * NT], BF16)
        PTs2 = sb("PTs2", [128, 2 * NT], BF16)
        rec = sb("rec", [128, NH * QB])
        Oacc = sb("Oacc", [128, NH * QB, KVL])
        Lacc = sb("Lacc", [128, NH * QB])
        assert NH * QB * KVL >= DC * PW
        pf = Oacc[:].rearrange("p a b -> p (a b)")[:, 0:DC * PW].rearrange("p (k w) -> p k w", k=DC)
        pa = ag[:, :, 0:PW]
        pb = uT[:].rearrange("p a b -> p (a b)")[:, 0:DC * PW].rearrange("p (k w) -> p k w", k=DC)
        pdiff = qabsT[:].rearrange("p a b c -> p (a b c)")[:, 0:DC * NT].rearrange("p (k w) -> p k w", k=DC)
        sview = lambda buf: buf[:, :, 0:cfg.dseq * (PS + cfg.dtok)].rearrange("p k (s c) -> p k s c", c=PS + cfg.dtok)
        spf, spa, spb = sview(pf), sview(pa), sview(pb)
        Onb = sb("Onb", [128, KVL], BF16)
        OT = sb("OT", [128, 2, NH, NT], BF16)
        attT = sb("attT", [64, NH, NT], BF16)

        NPG = cfg.dseq * cfg.npages
        pt_i = sb("pt_i", [128, NPG], I32)
        pt_f = sb("pt_f", [128, NPG])
        idx32 = sb("idx32", [128, NPG], I32)
        riota = sb("riota", [128, 1])
        smask = sb("smask_s", [4, NH * 4])
        lp32 = sb("lp32", [128, KVL])
        rp32 = sb("rp32", [128, ROPE])
        lpb = sb("lpb", [128, KVL], BF16)
        rp48 = sb("rp48", [128, 48], BF16)
        knew = sb("knew", [4, KVL], BF16)

        out_tickets = []

        def V(m, r, w, **kw): return pr.op("vector", m, r, w, **kw)
        def G(m, r, w, **kw): return pr.op("gpsimd", m, r, w, **kw)
        def A(m, r, w, **kw): return pr.op("scalar", m, r, w, **kw)
        def T(r, w, **kw): return pr.op("tensor", "matmul", r, w, **kw)
        def DMA(r, w, **kw): return pr.op("sync", "dma_start", r, w, dma=True, **kw)

        V("memset", [], ["ones"], ap=ones[:], constant=1.0)
        DMA([], ["nrm"], out=nrm[:], in_=norms)
        DMA([], ["psc"], out=psc[:], in_=pool_sc)
        DMA([], ["icnt"], out=icnt[:], in_=invcnt)
        V("memset", [], ["phalo"], ap=phalo[:], constant=0.0)

        def rmsnorm(n, which, dst, dst_name):
            G("tensor_tensor", ["x"], ["sq"], out=sq[:, :, :n], in0=x[:, :, :n], in1=x[:, :, :n], op=ALU.mult)
            for k in range(DC):
                T(["ones", "sq"], ["PB"], out=ps_r[:, :n], lhsT=ones[:], rhs=sq[:, k, :n], start=(k == 0), stop=(k == DC - 1))
            V("tensor_scalar", ["PB"], ["rstd"], out=rstd[:, :n], in0=ps_r[:, :n], scalar1=1.0 / D, scalar2=EPS,
              op0=ALU.mult, op1=ALU.add)
            A("activation", ["rstd"], ["rstd"], out=rstd[:, :n], in_=rstd[:, :n], func=AF.Sqrt)
            V("reciprocal", ["rstd"], ["rstd"], out=rstd[:, :n], in_=rstd[:, :n])
            for k in range(DC):
                V("scalar_tensor_tensor", ["x", "nrm", "rstd"], [dst_name], out=dst(k), in0=x[:, k, :n],
                  scalar=nrm[:, which, k:k + 1], in1=rstd[:, :n], op0=ALU.mult, op1=ALU.mult)

        def ffn(n, widx, which):
            rmsnorm(n, which, lambda k: xn[:, k, :n], "xn")
            for g in range(FC // 2):
                f0 = g * 256
                DMA([], ["wg_st"], out=wg_st[:], in_=wg[widx].rearrange("(kc p) f -> p kc f", p=128)[:, :, f0:f0 + 256])
                DMA([], ["wu_st"], out=wu_st[:], in_=wu[widx].rearrange("(kc p) f -> p kc f", p=128)[:, :, f0:f0 + 256])
                DMA([], ["wd_st"], out=wd_st[:], in_=wd[widx, f0:f0 + 256, :].rearrange("(c p) d -> p c d", p=128))
                G("tensor_copy", ["wg_st"], ["wg_b"], out=wg_b[:], in_=wg_st[:])
                A("copy", ["wu_st"], ["wu_b"], out=wu_b[:], in_=wu_st[:])
                G("tensor_copy", ["wd_st"], ["wd_b"], out=wd_b[:], in_=wd_st[:])
                for c in range(2):
                    for k in range(DC):
                        T(["wg_b", "xn"], ["PC"], out=ps_g[:, :n], lhsT=wg_b[:, k, c * 128:(c + 1) * 128], rhs=xn[:, k, :n],
                          start=(k == 0), stop=(k == DC - 1))
                    for k in range(DC):
                        T(["wu_b", "xn"], ["PD"], out=ps_u[:, :n], lhsT=wu_b[:, k, c * 128:(c + 1) * 128], rhs=xn[:, k, :n],
                          start=(k == 0), stop=(k == DC - 1))
                    A("activation", ["PC"], ["hs"], out=hs[:, c, :n], in_=ps_g[:, :n], func=AF.Silu)
                    V("tensor_tensor", ["hs", "PD"], ["hb"], out=hb[:, c, :n], in0=hs[:, c, :n], in1=ps_u[:, :n], op=ALU.mult)
                for half in range(2):
                    for dc4 in range(4):
                        dc = half * 4 + dc4
                        for c in range(2):
                            T(["wd_b", "hb"], ["PA"], out=ps_y[:, dc4, :n], lhsT=wd_b[:, c, dc * 128:(dc + 1) * 128],
                              rhs=hb[:, c, :n], start=(c == 0), stop=(c == 1))
                    for dc4 in range(4):
                        dc = half * 4 + dc4
                        V("scalar_tensor_tensor", ["PA", "x"], ["x"], out=x[:, dc, :n], in0=ps_y[:, dc4, :n], scalar=0.5,
                          in1=x[:, dc, :n], op0=ALU.mult, op1=ALU.add)

        def pool_layer(n, o, lidx, kind, t):
            sample = kind == "s"
            barrier()
            if not sample:
                F, A_, B_, Fn, An, Bn = pf, pa, pb, "Oacc", "ag", "uT"
                W = PS + n
                fv = lambda buf, k, lo, hi: buf[:, k, lo:hi]
                G("tensor_copy", ["phalo"], ["Oacc"], out=pf[:, :, 0:PS], in_=phalo[:, o, :, :])
                rmsnorm(n, 4 + lidx, lambda k: pf[:, k, PS:PS + n], "Oacc")
                G("tensor_copy", ["Oacc"], ["phalo"], out=phalo[:, o, :, :], in_=pf[:, :, n:n + PS])
                if t == cfg.ntiles - 1:
                    for k in range(DC):
                        out_tickets.append(DMA(["Oacc"], [], out=poolT[o, k * 128:(k + 1) * 128, 0:PS], in_=pf[:, k, n:n + PS]))
            else:
                F, A_, B_, Fn, An, Bn = spf, spa, spb, "Oacc", "ag", "uT"
                W = PS + cfg.dtok
                fv = lambda buf, k, lo, hi: buf[:, k, :, lo:hi]
                for k in range(DC):
                    DMA([], ["Oacc"], out=spf[:, k, :, 0:PS], in_=pool_hist[o, k * 128:(k + 1) * 128, :, :])
                rmsnorm(n, 4 + lidx, lambda k: spf[:, k, :, PS:PS + cfg.dtok], "Oacc")
                for k in range(DC):
                    out_tickets.append(DMA(["Oacc"], [], out=poolT[o, k * 128:(k + 1) * 128, PS:].rearrange("p (s c) -> p s c", c=PS),
                                           in_=spf[:, k, :, cfg.dtok:cfg.dtok + PS]))
            for gi in range(4):
                for k in (2 * gi, 2 * gi + 1):
                    src, srcn = F, Fn
                    sh = 1
                    for step in range(gi + 1):
                        dst, dstn = (A_, An) if step % 2 == 0 else (B_, Bn)
                        lo = 2 * sh - 1
                        V("tensor_tensor", [srcn], [dstn], out=fv(dst, k, lo, W), in0=fv(src, k, lo, W),
                          in1=fv(src, k, lo - sh, W - sh), op=ALU.add)
                        src, srcn = dst, dstn
                        sh *= 2
                    w = 2 ** (gi + 1)
                    if (not sample) and t == 0:
                        V("tensor_tensor", [srcn, "icnt"], [srcn], out=fv(src, k, PS, W), in0=fv(src, k, PS, W),
                          in1=icnt[:, gi, :n], op=ALU.mult)
                        V("tensor_tensor", [srcn, Fn], ["qabsT"], out=pdiff[:, k, :n], in0=fv(src, k, PS, W),
                          in1=fv(F, k, PS, W), op=ALU.subtract)
                    else:
                        dview = pdiff[:, k, :n] if not sample else pdiff[:, k, :n].rearrange("p (s c) -> p s c", c=cfg.dtok)
                        V("scalar_tensor_tensor", [srcn, Fn], ["qabsT"], out=dview, in0=fv(src, k, PS, W), scalar=1.0 / w,
                          in1=fv(F, k, PS, W), op0=ALU.mult, op1=ALU.subtract)
            for gi in range(4):
                for cc in range(2):
                    DMA([], ["pw_st"], out=pw_st[:, cc, :], in_=pool_w[o, gi, cc * 128:(cc + 1) * 128, :])
                G("tensor_copy", ["pw_st"], ["pw_b"], out=pw_b[:, gi, :, :], in_=pw_st[:])
            for half in range(2):
                for dc4 in range(4):
                    dc = half * 4 + dc4
                    gi, dd = dc // 2, dc % 2
                    for cc in range(2):
                        T(["pw_b", "qabsT"], ["PA"], out=ps_y[:, dc4, :n], lhsT=pw_b[:, gi, cc, dd * 128:(dd + 1) * 128],
                          rhs=pdiff[:, 2 * gi + cc, :n], start=(cc == 0), stop=(cc == 1))
                for dc4 in range(4):
                    dc = half * 4 + dc4
                    V("scalar_tensor_tensor", ["PA", "x", "psc"], ["x"], out=x[:, dc, :n], in0=ps_y[:, dc4, :n],
                      scalar=psc[:, o, dc:dc + 1], in1=x[:, dc, :n], op0=ALU.mult, op1=ALU.add)
            barrier()

        dummy = sb("dummy_bar", [128, 1])
        DMA([], ["qng"], out=qng[:], in_=qng_d)
        DMA([], ["kvng"], out=kvng[:], in_=kvng_d)
        DMA([], ["cw"], out=cw[:], in_=cw_d)
        DMA([], ["cvec"], out=cvec[:], in_=cvec_d)
        DMA([], ["amask"], out=amask[:], in_=amask_d)
        DMA([], ["ident32"], out=ident32[:], in_=ident_d)
        V("tensor_copy", ["ident32"], ["identb"], out=identb[:], in_=ident32[:])
        V("memset", [], ["ones2"], ap=ones2[:], constant=1.0)
        V("memset", [], ["chalo"], ap=chalo[:], constant=0.0)
        V("memset", [], ["kr32"], ap=kr32[:], constant=0.0)
        V("memset", [], ["rq32"], ap=rq32[:], constant=0.0)

        DMA([], ["pt_i"], out=pt_i[:], in_=pt_d)
        DMA([], ["riota"], out=riota[:], in_=riota_d)
        DMA([], ["smask"], out=smask[:], in_=smask_d)
        V("tensor_copy", ["pt_i"], ["pt_f"], out=pt_f[:], in_=pt_i[:])
        V("tensor_scalar", ["pt_f", "riota"], ["pt_f"], out=pt_f[:], in0=pt_f[:], scalar1=float(PAGE), scalar2=riota[:, 0:1],
          op0=ALU.mult, op1=ALU.add)
        V("tensor_copy", ["pt_f"], ["idx32"], out=idx32[:], in_=pt_f[:])
        V("memset", [], ["rp48"], ap=rp48[:], constant=0.0)

        def GATHER(w, out, src, col):
            return pr.op("gpsimd", "indirect_dma_start", ["idx32"], w, dma=True, out=out, out_offset=None, in_=src,
                         in_offset=bass.IndirectOffsetOnAxis(ap=idx32[:, col:col + 1], axis=0))

        def barrier():
            V("memset", [], ["PHASE"], ap=dummy[:], constant=0.0)

        def TR(r, w, **kw):
            return pr.op("tensor", "transpose", r, w, **kw)

        def rms_small(src, srcn, nch, width, n):
            G("tensor_tensor", [srcn], ["sq"], out=sq[:, 0:nch, :n], in0=src[:, 0:nch, :n], in1=src[:, 0:nch, :n], op=ALU.mult)
            for k in range(nch):
                T(["ones", "sq"], ["PB"], out=PB[:, :n], lhsT=ones[:], rhs=sq[:, k, :n], start=(k == 0), stop=(k == nch - 1))
            V("tensor_scalar", ["PB"], ["rstd"], out=rstd[:, :n], in0=PB[:, :n], scalar1=1.0 / width, scalar2=EPS,
              op0=ALU.mult, op1=ALU.add)
            A("activation", ["rstd"], ["rstd"], out=rstd[:, :n], in_=rstd[:, :n], func=AF.Sqrt)
            V("reciprocal", ["rstd"], ["rstd"], out=rstd[:, :n], in_=rstd[:, :n])

        def rope_from(srcA_a, srcA_b, srcB_a, srcB_b, srcn, dstA, dstB, dstn, n):
            V("tensor_tensor", [srcn, "cs"], ["kt1"], out=kt1[0:16, :n], in0=srcA_a, in1=cs[0:16, 0, :n], op=ALU.mult)
            V("tensor_tensor", [srcn, "cs"], ["kt2"], out=kt2[0:16, :n], in0=srcA_b, in1=cs[0:16, 1, :n], op=ALU.mult)
            V("tensor_tensor", ["kt1", "kt2"], [dstn], out=dstA, in0=kt1[0:16, :n], in1=kt2[0:16, :n], op=ALU.subtract)
            V("tensor_tensor", [srcn, "cs"], ["kt1"], out=kt1[32:48, :n], in0=srcB_b, in1=cs[32:48, 0, :n], op=ALU.mult)
            V("tensor_tensor", [srcn, "cs"], ["kt2"], out=kt2[32:48, :n], in0=srcB_a, in1=cs[32:48, 1, :n], op=ALU.mult)
            V("tensor_tensor", ["kt1", "kt2"], [dstn], out=dstB, in0=kt1[32:48, :n], in1=kt2[32:48, :n], op=ALU.add)

        def even_layer(n, e, lidx, kind, t, off):
            sample = kind == "s"
            QBn = max(n // 128, 1)
            barrier()
            rmsnorm(n, 4 + lidx, lambda k: xn[:, k, :n], "xn")
            DMA([], ["cs"], out=cs[:, :, :n], in_=cs_d[:, :, off:off + n].rearrange("a p n -> p a n"))
            win = w_in[e].rearrange("(kc p) f -> p kc f", p=128)
            dests = {}
            for j in range(6):
                dests[128 * j] = (qlat[:, j, :n], "qlat")
            for j in range(2):
                dests[768 + 128 * j] = (kvlat[:, j, :n], "kvlat")
            for j in range(8):
                dests[1056 + 128 * j] = (ag[:, j, :n], "ag")
            for g0 in (0, 256, 512, 768, 1056, 1312, 1568, 1824):
                DMA([], ["wg_st"], out=wg_st[:], in_=win[:, :, g0:g0 + 256])
                G("tensor_copy", ["wg_st"], ["wg_b"], out=wg_b[:], in_=wg_st[:])
                for c in range(2):
                    dst, dn = dests[g0 + 128 * c]
                    for k in range(DC):
                        T(["wg_b", "xn"], ["PC"], out=PC[:, :n], lhsT=wg_b[:, k, c * 128:(c + 1) * 128], rhs=xn[:, k, :n],
                          start=(k == 0), stop=(k == DC - 1))
                    V("tensor_copy", ["PC"], [dn], out=dst, in_=PC[:, :n])
            DMA([], ["wu_st"], out=wu_st[:, :, 0:32], in_=win[:, :, 1024:1056])
            G("tensor_copy", ["wu_st"], ["wu_b"], out=wu_b[:, :, 0:32], in_=wu_st[:, :, 0:32])
            for (p0, half, c0) in ((0, 0, 0), (0, 1, 16), (32, 0, 0), (32, 1, 16)):
                for k in range(DC):
                    T(["wu_b", "xn"], ["PD"], out=PD[p0:p0 + 16, half * n:(half + 1) * n], lhsT=wu_b[:, k, c0:c0 + 16],
                      rhs=xn[:, k, :n], start=(k == 0), stop=(k == DC - 1))
            rope_from(PD[0:16, 0:n], PD[0:16, n:2 * n], PD[32:48, 0:n], PD[32:48, n:2 * n], "PD",
                      kr32[0:16, :n], kr32[32:48, :n], "kr32", n)
            out_tickets.append(DMA(["kr32"], [], out=krT[e, 0:16, off:off + n], in_=kr32[0:16, :n]))
            out_tickets.append(DMA(["kr32"], [], out=krT[e, 16:32, off:off + n], in_=kr32[32:48, :n]))
            V("tensor_copy", ["kr32"], ["krb"], out=krb[:, :n], in_=kr32[:, :n])
            rms_small(qlat, "qlat", 6, float(QL), n)
            for j in range(6):
                V("scalar_tensor_tensor", ["qlat", "qng", "rstd"], ["qn"], out=qn[:, j, :n], in0=qlat[:, j, :n],
                  scalar=qng[:, e, j:j + 1], in1=rstd[:, :n], op0=ALU.mult, op1=ALU.mult)
            rms_small(kvlat, "kvlat", 2, float(KVL), n)
            for j in range(2):
                V("scalar_tensor_tensor", ["kvlat", "kvng", "rstd"], ["ckv32"], out=ckv32[:, j, :n], in0=kvlat[:, j, :n],
                  scalar=kvng[:, e, j:j + 1], in1=rstd[:, :n], op0=ALU.mult, op1=ALU.mult)
                out_tickets.append(DMA(["ckv32"], [], out=latT[e, j * 128:(j + 1) * 128, off:off + n], in_=ckv32[:, j, :n]))
            V("tensor_copy", ["ckv32"], ["ckvb"], out=ckvb[:, :, :n], in_=ckv32[:, :, :n])
            wuq = w_uq[e].rearrange("(kc p) f -> p kc f", p=128)
            for g in range(4):
                DMA([], ["wg_st"], out=wg_st[:, 0:6, 0:192], in_=wuq[:, :, 192 * g:192 * g + 192])
                G("tensor_copy", ["wg_st"], ["wq_b"], out=wq_b[:], in_=wg_st[:, 0:6, 0:192])
                for hh in range(2):
                    h = 2 * g + hh
                    c0 = 96 * hh
                    for k in range(6):
                        T(["wq_b", "qn"], ["PC"], out=PC[0:64, :n], lhsT=wq_b[:, k, c0:c0 + 64], rhs=qn[:, k, :n],
                          start=(k == 0), stop=(k == 5))
                    V("tensor_copy", ["PC"], ["qnope"], out=qnope[0:64, h, :n], in_=PC[0:64, :n])
                    for (p0, half, cc0) in ((0, 0, 64), (0, 1, 80), (32, 0, 64), (32, 1, 80)):
                        for k in range(6):
                            T(["wq_b", "qn"], ["PD"], out=PD[p0:p0 + 16, half * n:(half + 1) * n],
                              lhsT=wq_b[:, k, c0 + cc0:c0 + cc0 + 16], rhs=qn[:, k, :n], start=(k == 0), stop=(k == 5))
                    rope_from(PD[0:16, 0:n], PD[0:16, n:2 * n], PD[32:48, 0:n], PD[32:48, n:2 * n], "PD",
                              rq32[0:16, h, :n], rq32[32:48, h, :n], "rq32", n)
            V("tensor_copy", ["rq32"], ["rqb"], out=rqb[:, :, :n], in_=rq32[:, :, :n])
            DMA([], ["wu_st"], out=wu_st[0:64, :, :], in_=w_ukT[e].rearrange("h d c -> d h c"))
            G("tensor_copy", ["wu_st"], ["wuk_b"], out=wuk_b[:], in_=wu_st[0:64, :, :])
            for h in range(NH):
                for cc in range(2):
                    T(["wuk_b", "qnope"], ["PC"], out=PC[:, :n], lhsT=wuk_b[0:64, h, cc * 128:(cc + 1) * 128],
                      rhs=qnope[0:64, h, :n], start=True, stop=True)
                    V("tensor_copy", ["PC"], ["qabsT"], out=qabsT[:, cc, h, :n], in_=PC[:, :n])
            A("activation", ["ag"], ["ag"], out=ag[:, 4:8, :n], in_=ag[:, 4:8, :n], func=AF.Sigmoid)
            if not sample:
                G("tensor_copy", ["chalo"], ["uT"], out=uT[:, :, 0:HW_], in_=chalo[:, e, :, :])
                V("tensor_tensor", ["ag"], ["uT"], out=uT[:, :, HW_:HW_ + n], in0=ag[:, 0:4, :n], in1=ag[:, 4:8, :n], op=ALU.mult)
                G("tensor_copy", ["uT"], ["chalo"], out=chalo[:, e, :, :], in_=uT[:, :, n:n + HW_])
                if t == cfg.ntiles - 1:
                    for j in range(4):
                        out_tickets.append(DMA(["uT"], [], out=convT[e, j * 128:(j + 1) * 128, 0:HW_], in_=uT[:, j, n:n + HW_]))
                uview = lambda j, tap: uT[:, j, tap:tap + n]
                cview = lambda j: cacc[:, j, :n]
            else:
                LS = HW_ + cfg.dtok
                su = uT[:, :, 0:cfg.dseq * LS].rearrange("p j (s c) -> p j s c", c=LS)
                for j in range(4):
                    DMA([], ["uT"], out=su[:, j, :, 0:HW_], in_=conv_hist[e, j * 128:(j + 1) * 128, :, :])
                    V("tensor_tensor", ["ag"], ["uT"], out=su[:, j, :, HW_:LS],
                      in0=ag[:, j, :n].rearrange("p (s c) -> p s c", c=cfg.dtok),
                      in1=ag[:, 4 + j, :n].rearrange("p (s c) -> p s c", c=cfg.dtok), op=ALU.mult)
                    out_tickets.append(DMA(["uT"], [], out=convT[e, j * 128:(j + 1) * 128, HW_:].rearrange("p (s c) -> p s c", c=HW_),
                                           in_=su[:, j, :, cfg.dtok:LS]))
                uview = lambda j, tap: su[:, j, :, tap:tap + cfg.dtok]
                cview = lambda j: cacc[:, j, :n].rearrange("p (s c) -> p s c", c=cfg.dtok)
            for j in range(4):
                V("tensor_scalar", ["uT", "cw", "cvec"], ["cacc"], out=cview(j), in0=uview(j, 0), scalar1=cw[:, e, j, 0:1],
                  scalar2=cvec[:, 0, e, j:j + 1], op0=ALU.mult, op1=ALU.add)
                for tap in range(1, CW):
                    V("scalar_tensor_tensor", ["uT", "cw", "cacc"], ["cacc"], out=cview(j), in0=uview(j, tap),
                      scalar=cw[:, e, j, tap:tap + 1], in1=cview(j), op0=ALU.mult, op1=ALU.add)
            for j in range(4):
                T(["ones", "cacc"], ["PB"], out=PB[:, :n], lhsT=ones[:], rhs=cacc[:, j, :n], start=(j == 0), stop=(j == 3))
            V("tensor_scalar", ["PB"], ["cmean"], out=cmean[:, :n], in0=PB[:, :n], scalar1=1.0 / CCH, scalar2=None, op0=ALU.mult)
            for j in range(4):
                V("tensor_tensor", ["cacc", "cmean"], ["cacc"], out=cacc[:, j, :n], in0=cacc[:, j, :n], in1=cmean[:, :n],
                  op=ALU.subtract)
            rms_small(cacc, "cacc", 4, float(CCH), n)
            for j in range(4):
                V("tensor_tensor", ["cacc", "rstd"], ["cacc"], out=cacc[:, j, :n], in0=cacc[:, j, :n], in1=rstd[:, :n], op=ALU.mult)
                A("activation", ["cacc", "cvec"], ["cv"], out=cv[:, j, :n], in_=cacc[:, j, :n], func=AF.Silu,
                  scale=cvec[:, 1, e, j:j + 1], bias=cvec[:, 2, e, j:j + 1])
            if not sample:
                DMA(["ckvb"], ["kcT"], out=kcT[e][:, :, off:off + n], in_=ckvb[:, :, :n])
                DMA(["krb"], ["kcR"], out=kcR[e][:, off:off + n], in_=krb[:, :n])
                for blk in range(QBn):
                    for cc in range(2):
                        TR(["ckvb", "identb"], ["PT_"], out=PT_[:, cc * 128:(cc + 1) * 128],
                           in_=ckvb[:, cc, blk * 128:(blk + 1) * 128], identity=identb[:])
                    V("tensor_copy", ["PT_"], ["kaug"], out=kaug[:, blk, :], in_=PT_[:, 0:KVL])
                    DMA(["kaug"], ["kcA"], out=kcA[e, off // 128 + blk], in_=kaug[:, blk, :])
                nkb = (off + n) // 128
                barrier()
                it = 0
                for j in range(nkb):
                    DMA(["kcT"], ["kbT"], out=kbT[:], in_=kcT[e][:, :, j * 128:(j + 1) * 128])
                    DMA(["kcR"], ["kbR"], out=kbR[:], in_=kcR[e][:, j * 128:(j + 1) * 128])
                    DMA(["kcA"], ["kbA"], out=kbA[:], in_=kcA[e, j])
                    jj = j - off // 128
                    for h in range(0, NH, 2):
                        p = it % 2
                        it += 1
                        PS_, psn = (PC, "PC") if p == 0 else (PD, "PD")
                        PTx, ptn = (PTs, "PTs") if p == 0 else (PTs2, "PTs2")
                        pan, pbn = f"PAh{p}", "PBh"
                        T(["kbT", "qabsT"], [psn], out=PS_[:, 0:2 * n], lhsT=kbT[:, 0, :], rhs=qabsT[:, 0, h:h + 2, :n], start=True, stop=False)
                        T(["kbT", "qabsT"], [psn], out=PS_[:, 0:2 * n], lhsT=kbT[:, 1, :], rhs=qabsT[:, 1, h:h + 2, :n], start=False, stop=False)
                        T(["kbR", "rqb"], [psn], out=PS_[:, 0:2 * n], lhsT=kbR[:, :], rhs=rqb[:, h:h + 2, :n], start=False, stop=True)
                        A("activation", [psn], [ptn], out=PTx[:, 0:2 * n], in_=PS_[:, 0:2 * n], func=AF.Exp, scale=SCALE)
                        if jj >= 0:
                            for hh in range(2):
                                V("tensor_tensor", [ptn, "amask"], [ptn], out=PTx[:, hh * n:(hh + 1) * n], in0=PTx[:, hh * n:(hh + 1) * n],
                                  in1=amask[:, jj, :n], op=ALU.mult)
                        for hh in range(2):
                            for qb in range(QBn):
                                i4 = hh * QBn + qb
                                lt = PTx[:, hh * n + qb * 128:hh * n + (qb + 1) * 128]
                                T([ptn, "kbA"], [pan], out=PA[:, 4 * p + i4, :], lhsT=lt, rhs=kbA[:, :], start=True, stop=True)
                                T([ptn, "ones2"], [pbn], out=PB[:, 2 * i4:2 * i4 + 2], lhsT=lt, rhs=ones2[:],
                                  start=True, stop=True)
                        c2 = 2 * QBn
                        lview = PB[:, 0:2 * c2].rearrange("p (i two) -> p i two", two=2)[:, :, 0]
                        osl = Oacc[:, h * QBn:h * QBn + c2, :]
                        lsl = Lacc[:, h * QBn:h * QBn + c2]
                        if j == 0:
                            V("tensor_copy", [pan], ["Oacc"], out=osl, in_=PA[:, 4 * p:4 * p + c2, :])
                            V("tensor_copy", [pbn], ["Lacc"], out=lsl, in_=lview)
                        else:
                            V("tensor_tensor", [pan, "Oacc"], ["Oacc"], out=osl, in0=PA[:, 4 * p:4 * p + c2, :], in1=osl, op=ALU.add)
                            V("tensor_tensor", [pbn, "Lacc"], ["Lacc"], out=lsl, in0=lview, in1=lsl, op=ALU.add)
                barrier()
                V("reciprocal", ["Lacc"], ["rec"], out=rec[:, 0:NH * QBn], in_=Lacc[:, 0:NH * QBn])
                for h in range(NH):
                    for qb in range(QBn):
                        idx = h * QBn + qb
                        V("tensor_scalar", ["Oacc", "rec"], ["Onb"], out=Onb[:], in0=Oacc[:, idx, :], scalar1=rec[:, idx:idx + 1],
                          scalar2=None, op0=ALU.mult)
                        for cc in range(2):
                            TR(["Onb", "identb"], ["PT_"], out=PT_[:, cc * 128:(cc + 1) * 128],
                               in_=Onb[:, cc * 128:(cc + 1) * 128], identity=identb[:])
                        V("tensor_copy", ["PT_"], ["OT"], out=OT[:, :, h, qb * 128:(qb + 1) * 128],
                          in_=PT_[:, 0:256].rearrange("p (c q) -> p c q", c=2))
            else:
                dtk = cfg.dtok
                NQ = NH * dtk
                for sq_ in range(cfg.dseq):
                    t0 = sq_ * dtk
                    qa = lambda cc: qabsT[:, cc, :, t0:t0 + dtk]
                    for pg in range(cfg.npages + 1):
                        new = pg == cfg.npages
                        if not new:
                            col = sq_ * cfg.npages + pg
                            GATHER(["lp32"], lp32[:, :], cache_lat[e][:, :], col)
                            GATHER(["rp32"], rp32[:, :], cache_kr[e][:, :], col)
                            V("tensor_copy", ["lp32"], ["lpb"], out=lpb[:], in_=lp32[:])
                            V("tensor_copy", ["rp32"], ["rp48"], out=rp48[:, 0:16], in_=rp32[:, 0:16])
                            V("tensor_copy", ["rp32"], ["rp48"], out=rp48[:, 32:48], in_=rp32[:, 16:32])
                            for cc in range(2):
                                TR(["lpb", "identb"], ["PT_"], out=PT_[:, cc * 128:(cc + 1) * 128], in_=lpb[:, cc * 128:(cc + 1) * 128],
                                   identity=identb[:])
                            TR(["rp48", "identb"], ["PT_"], out=PT_[0:48, 256:384], in_=rp48[:, :], identity=identb[:])
                            V("tensor_copy", ["PT_"], ["kbT"], out=kbT[:], in_=PT_[:, 0:256].rearrange("p (c q) -> p c q", c=2))
                            V("tensor_copy", ["PT_"], ["kbR"], out=kbR[:], in_=PT_[0:48, 256:384])
                            KP = 128
                            lk = lambda cc: kbT[:, cc, :]
                            lr = kbR[:, :]
                            vrhs, vn = lpb[:, :], "lpb"
                        else:
                            KP = dtk
                            lk = lambda cc: ckvb[:, cc, t0:t0 + dtk]
                            lr = krb[:, t0:t0 + dtk]
                            for cc in range(2):
                                TR(["ckvb", "identb"], ["PT_"], out=PT_[0:dtk, cc * 128:(cc + 1) * 128], in_=ckvb[:, cc, t0:t0 + dtk],
                                   identity=identb[:])
                            V("tensor_copy", ["PT_"], ["knew"], out=knew[0:dtk, :], in_=PT_[0:dtk, 0:256])
                            vrhs, vn = knew[0:dtk, :], "knew"
                        T(["kbT", "ckvb", "qabsT"], ["PC"], out=PC[0:KP, 0:NQ], lhsT=lk(0), rhs=qa(0), start=True, stop=False)
                        T(["kbT", "ckvb", "qabsT"], ["PC"], out=PC[0:KP, 0:NQ], lhsT=lk(1), rhs=qa(1), start=False, stop=False)
                        T(["kbR", "krb", "rqb"], ["PC"], out=PC[0:KP, 0:NQ], lhsT=lr, rhs=rqb[:, :, t0:t0 + dtk], start=False, stop=True)
                        A("activation", ["PC"], ["PTs"], out=PTs[0:KP, 0:NQ], in_=PC[0:KP, 0:NQ], func=AF.Exp, scale=SCALE)
                        if new:
                            V("tensor_tensor", ["PTs", "smask"], ["PTs"], out=PTs[0:KP, 0:NQ], in0=PTs[0:KP, 0:NQ], in1=smask[0:KP, :],
                              op=ALU.mult)
                        T(["PTs", vn], ["PA"], out=PA[0:NQ, 0, :], lhsT=PTs[0:KP, 0:NQ], rhs=vrhs, start=True, stop=True)
                        T(["PTs", "ones2"], ["PB"], out=PB[0:NQ, 0:2], lhsT=PTs[0:KP, 0:NQ], rhs=ones2[0:KP, :], start=True, stop=True)
                        if pg == 0:
                            V("tensor_copy", ["PA"], ["Oacc"], out=Oacc[0:NQ, 0, :], in_=PA[0:NQ, 0, :])
                            V("tensor_copy", ["PB"], ["Lacc"], out=Lacc[0:NQ, 0:1], in_=PB[0:NQ, 0:1])
                        else:
                            V("tensor_tensor", ["PA", "Oacc"], ["Oacc"], out=Oacc[0:NQ, 0, :], in0=PA[0:NQ, 0, :], in1=Oacc[0:NQ, 0, :],
                              op=ALU.add)
                            V("tensor_tensor", ["PB", "Lacc"], ["Lacc"], out=Lacc[0:NQ, 0:1], in0=PB[0:NQ, 0:1], in1=Lacc[0:NQ, 0:1],
                              op=ALU.add)
                    V("reciprocal", ["Lacc"], ["rec"], out=rec[0:NQ, 0:1], in_=Lacc[0:NQ, 0:1])
                    V("tensor_scalar", ["Oacc", "rec"], ["Onb"], out=Onb[0:NQ, :], in0=Oacc[0:NQ, 0, :], scalar1=rec[0:NQ, 0:1],
                      scalar2=None, op0=ALU.mult)
                    for cc in range(2):
                        TR(["Onb", "identb"], ["PT_"], out=PT_[:, cc * NQ:(cc + 1) * NQ], in_=Onb[0:NQ, cc * 128:(cc + 1) * 128],
                           identity=identb[0:NQ, 0:NQ])
                    for cc in range(2):
                        V("tensor_copy", ["PT_"], ["OT"], out=OT[:, cc, :, t0:t0 + dtk],
                          in_=PT_[:, cc * NQ:(cc + 1) * NQ].rearrange("p (h t) -> p h t", t=dtk))
            DMA([], ["wd_st"], out=wd_st[:, :, 0:NH * VD], in_=w_uv[e].rearrange("(c p) f -> p c f", p=128))
            G("tensor_copy", ["wd_st"], ["wuv_b"], out=wuv_b[:], in_=wd_st[:, :, 0:NH * VD])
            for h in range(NH):
                for cc in range(2):
                    T(["wuv_b", "OT"], ["PC"], out=PC[0:64, :n], lhsT=wuv_b[:, cc, h * VD:(h + 1) * VD], rhs=OT[:, cc, h, :n],
                      start=(cc == 0), stop=(cc == 1))
                V("tensor_copy", ["PC"], ["attT"], out=attT[0:64, h, :n], in_=PC[0:64, :n])
            for dc in range(DC):
                DMA([], ["wg_st"], out=wg_st[0:64, :, 0:128],
                    in_=w_out[e, 0:NH * VD, dc * 128:(dc + 1) * 128].rearrange("(h v) d -> v h d", v=VD))
                DMA([], ["wu_st"], out=wu_st[:, 0:4, 0:128],
                    in_=w_out[e, NH * VD:D, dc * 128:(dc + 1) * 128].rearrange("(j p) d -> p j d", p=128))
                G("tensor_copy", ["wg_st"], ["woa_b"], out=woa_b[:], in_=wg_st[0:64, :, 0:128])
                G("tensor_copy", ["wu_st"], ["woc_b"], out=woc_b[:], in_=wu_st[:, 0:4, 0:128])
                for h in range(NH):
                    T(["woa_b", "attT"], ["PC"], out=PC[:, :n], lhsT=woa_b[0:64, h, :], rhs=attT[0:64, h, :n], start=(h == 0), stop=False)
                for j in range(4):
                    T(["woc_b", "cv"], ["PC"], out=PC[:, :n], lhsT=woc_b[:, j, :], rhs=cv[:, j, :n], start=False, stop=(j == 3))
                V("tensor_tensor", ["PC", "x"], ["x"], out=x[:, dc, :n], in0=PC[:, :n], in1=x[:, dc, :n], op=ALU.add)
            barrier()

        tiles = [("p", t, NT, t * NT) for t in range(cfg.ntiles)] + [("s", 0, NS, SEQ)]
        for kind, t, n, off in tiles:
            for k in range(DC):
                DMA([], ["x"], out=x[:, k, :n], in_=xT[k * 128:(k + 1) * 128, off:off + n])
            for l in range(n_layers):
                if "f" in cfg.parts:
                    ffn(n, 2 * l, l)
                if l % 2 == 1 and "p" in cfg.parts:
                    pool_layer(n, l // 2, l, kind, t)
                if l % 2 == 0 and "a" in cfg.parts:
                    even_layer(n, l // 2, l, kind, t, off)
                if "f" in cfg.parts:
                    ffn(n, 2 * l + 1, 8 + l)
            rmsnorm(n, 12, lambda k: sq[:, k, :n], "sq")
            for k in range(DC):
                out_tickets.append(DMA(["sq"], [], out=yT[k * 128:(k + 1) * 128, off:off + n], in_=sq[:, k, :n]))

        pr.finish("sync", out_tickets)
        with nc.Block() as block:
            pr.emit(block)
    return nc


def prep_core(inp, cfg, c):
    b = c % 2
    s0, s1 = c * cfg.dseq, (c + 1) * cfg.dseq
    f = lambda a: np.ascontiguousarray(np.asarray(a, dtype=np.float32))
    m = {}
    m["xT"] = f(np.concatenate([inp["x_prompt"][b].T, inp["x_sample"][s0:s1].reshape(-1, D).T], axis=1))
    nr = np.concatenate([inp["ffn1_norm"], inp["mix_norm"], inp["ffn2_norm"], inp["final_norm"][None]], 0)
    m["norms"] = f(nr.reshape(13, DC, 128).transpose(2, 0, 1))
    il = lambda a1, a2: np.stack([a1, a2], 1).reshape((2 * DEPTH,) + a1.shape[1:])
    m["wg"] = f(il(inp["ffn1_w_gate"], inp["ffn2_w_gate"]))
    m["wu"] = f(il(inp["ffn1_w_up"], inp["ffn2_w_up"]))
    m["wd"] = f(il(inp["ffn1_w_down"], inp["ffn2_w_down"]))
    m["pool_w"] = f(inp["pool_w"])
    m["pool_sc"] = f(inp["pool_scale"].reshape(2, DC, 128).transpose(2, 0, 1))
    m["pool_hist"] = f(inp["state_pool"][:, s0:s1].transpose(0, 3, 1, 2))
    pos = np.arange(cfg.nt) + 1
    ic = np.stack([1.0 / np.minimum(pos, w) for w in (2, 4, 8, 16)], 0)
    m["invcnt"] = f(np.broadcast_to(ic[None], (128, 4, cfg.nt)))
    m["w_in"] = f(inp["w_in"])
    m["qng"] = f(inp["q_norm"].reshape(2, 6, 128).transpose(2, 0, 1))
    m["kvng"] = f(inp["kv_norm"].reshape(2, 2, 128).transpose(2, 0, 1))
    m["w_uq"] = f(inp["w_uq"].reshape(2, QL, NH * (NOPE + ROPE)))
    m["w_ukT"] = f(inp["w_uk"].transpose(0, 2, 3, 1))
    m["w_uv"] = f(inp["w_uv"].reshape(2, KVL, NH * VD))
    m["cw"] = f(inp["conv_w"].transpose(0, 2, 1).reshape(2, 4, 128, CW).transpose(2, 0, 1, 3))
    cvv = np.stack([inp["conv_b"], inp["conv_ln_g"], inp["conv_ln_b"]], 0)
    m["cvec"] = f(cvv.reshape(3, 2, 4, 128).transpose(3, 0, 1, 2))
    m["w_out"] = f(inp["w_out"])
    past = cfg.npages * PAGE
    pos = np.concatenate([np.arange(cfg.seq, dtype=np.float32),
                          np.tile(past + np.arange(cfg.dtok, dtype=np.float32), cfg.dseq)]).astype(np.float32)
    inv = (np.float32(10000.0) ** (-np.arange(0, ROPE, 2, dtype=np.float32) / np.float32(ROPE))).astype(np.float32)
    ang = (pos[None, :] * inv[:, None]).astype(np.float32)
    cs = np.zeros((2, 48, pos.shape[0]), np.float32)
    for a, fn in enumerate((np.cos, np.sin)):
        cs[a, 0:16] = fn(ang)
        cs[a, 32:48] = fn(ang)
    m["cossin"] = cs
    m["conv_hist"] = f(inp["state_conv"][:, s0:s1].transpose(0, 3, 1, 2))
    qb = cfg.nt // 128
    key = np.arange(128)[:, None, None] + 128 * np.arange(qb)[None, :, None]
    qq = np.arange(cfg.nt)[None, None, :]
    m["amask"] = f((key <= qq).astype(np.float32))
    m["ident"] = np.eye(128, dtype=np.float32)
    for i in range(2):
        m[f"cache_lat{i}"] = f(inp["cache_latent"][i]).reshape(-1, KVL)
        m[f"cache_kr{i}"] = f(inp["cache_krope"][i]).reshape(-1, ROPE)
    pt = np.ascontiguousarray(np.asarray(inp["page_table"])[s0:s1].reshape(1, -1).astype(np.int32))
    m["pt_rep"] = np.ascontiguousarray(np.broadcast_to(pt, (128, pt.shape[1])))
    m["rowiota"] = np.arange(128, dtype=np.float32).reshape(128, 1)
    tk = np.arange(cfg.dtok)
    sm = (tk[:, None, None] <= tk[None, None, :]).astype(np.float32)
    m["smask"] = f(np.broadcast_to(sm, (cfg.dtok, NH, cfg.dtok)).reshape(cfg.dtok, NH * cfg.dtok))
    return m


def assemble(res, cfg, B, DB):
    SEQ, ds, dt = cfg.seq, cfg.dseq, cfg.dtok
    y_p = np.zeros((B, SEQ, D), np.float32)
    y_s = np.zeros((DB, dt, D), np.float32)
    pool_p = np.zeros((2, B, PS, D), np.float32)
    pool_s = np.zeros((2, DB, PS, D), np.float32)
    for c, r in enumerate(res):
        s0, s1 = c * ds, (c + 1) * ds
        if c < B:
            y_p[c] = r["yT"][:, :SEQ].T
            for o in range(2):
                pool_p[o, c] = r["poolT"][o][:, :PS].T
        y_s[s0:s1] = r["yT"][:, SEQ:].T.reshape(ds, dt, D)
        for o in range(2):
            pool_s[o, s0:s1] = r["poolT"][o][:, PS:].reshape(D, ds, PS).transpose(1, 2, 0)
    lat_p = np.zeros((2, B, SEQ, KVL), np.float32)
    kr_p = np.zeros((2, B, SEQ, ROPE), np.float32)
    lat_s = np.zeros((2, DB, dt, KVL), np.float32)
    kr_s = np.zeros((2, DB, dt, ROPE), np.float32)
    conv_p = np.zeros((2, B, CW - 1, CCH), np.float32)
    conv_s = np.zeros((2, DB, CW - 1, CCH), np.float32)
    H = CW - 1
    for c, r in enumerate(res):
        s0, s1 = c * ds, (c + 1) * ds
        for e in range(2):
            if c < B:
                lat_p[e, c] = r["latT"][e][:, :SEQ].T
                kr_p[e, c] = r["krT"][e][:, :SEQ].T
                conv_p[e, c] = r["convT"][e][:, :H].T
            lat_s[e, s0:s1] = r["latT"][e][:, SEQ:].T.reshape(ds, dt, KVL)
            kr_s[e, s0:s1] = r["krT"][e][:, SEQ:].T.reshape(ds, dt, ROPE)
            conv_s[e, s0:s1] = r["convT"][e][:, H:].reshape(CCH, ds, H).transpose(1, 2, 0)
    return dict(y_p=y_p, y_s=y_s, pool_p=pool_p, pool_s=pool_s, lat_p=lat_p, kr_p=kr_p, lat_s=lat_s, kr_s=kr_s,
                conv_p=conv_p, conv_s=conv_s)


def run_cfg(inputs, cfg, B, DB):
    inp = {k: np.asarray(v) for k, v in inputs.items()}
    nc = build(cfg)
    in_maps = [prep_core(inp, cfg, c) for c in range(cfg.ncores)]
    res = run_bass_kernel_spmd(nc, in_maps, core_ids=list(range(cfg.ncores)))
    o = assemble(res.results, cfg, B, DB)
    return (o["y_p"], o["y_s"], o["lat_p"], o["kr_p"], o["lat_s"], o["kr_s"], o["conv_p"], o["conv_s"], o["pool_p"], o["pool_s"])


def kernel(**inputs):
    cfg = Cfg(seq=8192, nt=256, dseq=16, dtok=4, npages=64, nphys=int(np.asarray(inputs["cache_latent"]).shape[1]), ncores=8)
    cfg.parts = "fpa"
    return run_cfg(inputs, cfg, 2, 128)
```
